# Optimizing a Trainium2 kernel written in Bass

```python
import math
import jax, jax.numpy as jnp
from jax import lax
import numpy as np

D_MODEL = 1024
BATCH = 8
SEQ = 2048
DEPTH = 2

N_MIXERS = 2
EXPAND = 2
BRANCH_WIDTH = EXPAND * D_MODEL
HG_HEAD_DIM = 128
HG_HEADS = BRANCH_WIDTH // HG_HEAD_DIM
HG_CHUNK = 16
SW_HEAD_DIM = 64
SW_Q_HEADS = BRANCH_WIDTH // SW_HEAD_DIM
SW_KV_HEADS = SW_Q_HEADS // 8
SW_GROUP = SW_Q_HEADS // SW_KV_HEADS
SW_Q_WIDTH = SW_Q_HEADS * SW_HEAD_DIM
SW_KV_WIDTH = SW_KV_HEADS * SW_HEAD_DIM
SW_WINDOW = 128
SW_BLOCK = 128
SW_SCALE = SW_HEAD_DIM ** -0.5
N_A = (DEPTH + 1) // 2
N_B = DEPTH // 2
DEEPNORM_ALPHA = (2.0 * DEPTH) ** 0.25
DEEPNORM_BETA = (8.0 * DEPTH) ** -0.25
LN_EPS = 1e-5
RMS_EPS = 1e-6
ADA_INIT = 0.2

kernel_name = 'hybrid_hgrn2_swa_sink_adaln_deepnorm'


def layer_norm(x, g, b):
    xf = x.astype(jnp.float32)
    mu = jnp.mean(xf, axis=-1, keepdims=True)
    var = jnp.mean(jnp.square(xf - mu), axis=-1, keepdims=True)
    y = (xf - mu) * lax.rsqrt(var + LN_EPS) * g.astype(jnp.float32) + b.astype(jnp.float32)
    return y.astype(x.dtype)


def rms_norm(x):
    return x * lax.rsqrt(jnp.mean(jnp.square(x), axis=-1, keepdims=True) + RMS_EPS)


def alibi_slopes(n):
    return jnp.asarray(2.0 ** (-8.0 * np.arange(1, n + 1) / n), dtype=jnp.float32)


def chunked_gated_recurrence(q, k, v, log_f):
    B, T, H, K = q.shape
    V = v.shape[-1]
    C = HG_CHUNK
    N = T // C

    def to_chunks(a):
        return a.reshape(B, N, C, H, a.shape[-1]).transpose(1, 0, 3, 2, 4)

    qc, kc, vc, gc = to_chunks(q), to_chunks(k), to_chunks(v), to_chunks(log_f)
    bc = jnp.cumsum(gc, axis=-2)
    causal = jnp.tril(jnp.ones((C, C), dtype=bool))[:, :, None]

    def step(S, xs):
        q_, k_, v_, b_ = xs
        b_last = b_[..., -1, :]
        inter = jnp.einsum('bhtk,bhkv->bhtv', q_ * jnp.exp(b_), S)
        rel = jnp.where(causal, b_[..., :, None, :] - b_[..., None, :, :], -jnp.inf)
        decay = jnp.exp(rel)
        scores = jnp.einsum('bhtk,bhsk,bhtsk->bhts', q_, k_, decay)
        intra = jnp.einsum('bhts,bhsv->bhtv', scores, v_)
        k_dec = k_ * jnp.exp(b_last[..., None, :] - b_)
        S = S * jnp.exp(b_last)[..., None] + jnp.einsum('bhsk,bhsv->bhkv', k_dec, v_)
        return S, inter + intra

    S0 = jnp.zeros((B, H, K, V), jnp.float32)
    _, out = lax.scan(step, S0, (qc, kc, vc, bc))
    return out.transpose(1, 0, 3, 2, 4).reshape(B, T, H, V)


def hgrn2_mixer(h, w_in, w_out, lower_bound, norm_w):
    B, T, _ = h.shape
    proj = jnp.einsum('btd,de->bte', h, w_in)
    q, fx, i, z = jnp.split(proj, 4, axis=-1)

    def heads(a):
        return a.reshape(B, T, HG_HEADS, HG_HEAD_DIM).astype(jnp.float32)

    q, fx, i = heads(q), heads(fx), heads(i)
    lb = lower_bound.astype(jnp.float32).reshape(HG_HEADS, HG_HEAD_DIM)
    f = lb + (1.0 - lb) * jax.nn.sigmoid(fx)
    log_f = jnp.log(f)
    k = (1.0 - lb) * jax.nn.sigmoid(-fx)
    o = chunked_gated_recurrence(q, k, i, log_f)
    o = rms_norm(o) * norm_w.astype(jnp.float32)
    o = o.reshape(B, T, BRANCH_WIDTH).astype(h.dtype) * jax.nn.silu(z)
    return jnp.einsum('bte,ed->btd', o, w_out)


def swa_mixer(h, w_in, w_out, sinks):
    B, T, _ = h.shape
    NB = T // SW_BLOCK
    proj = jnp.einsum('btd,de->bte', h, w_in)
    q, k, v, z = jnp.split(proj, [SW_Q_WIDTH, SW_Q_WIDTH + SW_KV_WIDTH, SW_Q_WIDTH + 2 * SW_KV_WIDTH], axis=-1)
    q = q.reshape(B, NB, SW_BLOCK, SW_KV_HEADS, SW_GROUP, SW_HEAD_DIM).transpose(1, 0, 2, 3, 4, 5)
    k = k.reshape(B, T, SW_KV_HEADS, SW_HEAD_DIM)
    v = v.reshape(B, T, SW_KV_HEADS, SW_HEAD_DIM)

    def key_blocks(a):
        prev = jnp.pad(a, ((0, 0), (SW_BLOCK, 0), (0, 0), (0, 0)))[:, :T]
        shp = (B, NB, SW_BLOCK, SW_KV_HEADS, SW_HEAD_DIM)
        kb = jnp.concatenate([prev.reshape(shp), a.reshape(shp)], axis=2)
        return kb.transpose(1, 0, 2, 3, 4)

    kb, vb = key_blocks(k), key_blocks(v)
    dist = jnp.arange(SW_BLOCK)[:, None] + SW_BLOCK - jnp.arange(2 * SW_BLOCK)[None, :]
    band = (dist >= 0) & (dist < SW_WINDOW)
    slopes = alibi_slopes(SW_Q_HEADS).reshape(SW_KV_HEADS, SW_GROUP)
    bias = -slopes[:, :, None, None] * dist.astype(jnp.float32)
    sink = sinks.astype(jnp.float32).reshape(SW_KV_HEADS, SW_GROUP)[None, :, :, None, None]

    def block(args):
        n, qn, kn, vn = args
        s = jnp.einsum('bqhgd,bkhd->bhgqk', qn, kn).astype(jnp.float32) * SW_SCALE + bias
        key_pos = n * SW_BLOCK - SW_BLOCK + jnp.arange(2 * SW_BLOCK)
        valid = band & (key_pos >= 0)[None, :]
        s = jnp.where(valid, s, -jnp.inf)
        m = jnp.maximum(jnp.max(s, axis=-1, keepdims=True), sink)
        p = jnp.exp(s - m)
        denom = jnp.sum(p, axis=-1, keepdims=True) + jnp.exp(sink - m)
        pr = (p / denom).astype(vn.dtype)
        return jnp.einsum('bhgqk,bkhd->bqhgd', pr, vn)

    out = lax.map(block, (jnp.arange(NB), q, kb, vb))
    out = out.transpose(1, 0, 2, 3, 4, 5).reshape(B, T, SW_Q_WIDTH)
    out = out.astype(h.dtype) * jax.nn.silu(z)
    return jnp.einsum('bte,ed->btd', out, w_out)


def setup_inputs(seed: int = 0) -> dict:
    key = jax.random.key(seed)
    ks = jax.random.split(key, 13)
    f32 = jnp.float32

    def nrm(k, shape, s):
        return s * jax.random.normal(k, shape, f32)

    sw_in_width = SW_Q_WIDTH + 2 * SW_KV_WIDTH + BRANCH_WIDTH
    return {
        'x': nrm(ks[0], (BATCH, SEQ, D_MODEL), 1.0),
        'c': nrm(ks[1], (BATCH, D_MODEL), 1.0),
        'ada_w': nrm(ks[2], (DEPTH, D_MODEL, 3 * D_MODEL), ADA_INIT * D_MODEL ** -0.5),
        'ada_b': nrm(ks[3], (DEPTH, 3 * D_MODEL), 0.02),
        'ln_g': 1.0 + nrm(ks[4], (DEPTH, D_MODEL), 0.02),
        'ln_b': nrm(ks[5], (DEPTH, D_MODEL), 0.02),
        'hg_w_in': nrm(ks[6], (N_A, D_MODEL, 4 * BRANCH_WIDTH), D_MODEL ** -0.5),
        'hg_w_out': nrm(ks[7], (N_A, BRANCH_WIDTH, D_MODEL), DEEPNORM_BETA * BRANCH_WIDTH ** -0.5),
        'hg_lb_logits': nrm(ks[8], (N_A + 1, BRANCH_WIDTH), 1.0),
        'hg_norm_w': 1.0 + nrm(ks[9], (N_A, HG_HEAD_DIM), 0.02),
        'sw_w_in': nrm(ks[10], (N_B, D_MODEL, sw_in_width), D_MODEL ** -0.5),
        'sw_w_out': nrm(ks[11], (N_B, SW_Q_WIDTH, D_MODEL), DEEPNORM_BETA * SW_Q_WIDTH ** -0.5),
        'sw_sinks': nrm(ks[12], (N_B, SW_Q_HEADS), 0.5),
    }


def reference(x, c, ada_w, ada_b, ln_g, ln_b, hg_w_in, hg_w_out, hg_lb_logits, hg_norm_w,
              sw_w_in, sw_w_out, sw_sinks):
    lower_bounds = jnp.cumsum(jax.nn.softmax(hg_lb_logits.astype(jnp.float32), axis=0), axis=0)
    c_act = jax.nn.silu(c)
    for layer in range(DEPTH):
        mod = jnp.einsum('bd,de->be', c_act, ada_w[layer]) + ada_b[layer]
        shift, scale, gate = jnp.split(mod[:, None, :], 3, axis=-1)
        h = x * (1.0 + scale) + shift
        j = layer // N_MIXERS
        if layer % N_MIXERS == 0:
            y = hgrn2_mixer(h, hg_w_in[j], hg_w_out[j], lower_bounds[j], hg_norm_w[j])
        else:
            y = swa_mixer(h, sw_w_in[j], sw_w_out[j], sw_sinks[j])
        x = layer_norm(DEEPNORM_ALPHA * x + (1.0 + gate) * y, ln_g[layer], ln_b[layer])
    return x
```

```python
import numpy as np
from contextlib import ExitStack
import concourse.bass as bass
import concourse.mybir as mybir
from concourse.bass_utils import run_bass_kernel_spmd

F32 = mybir.dt.float32
BF16 = mybir.dt.bfloat16
AF = mybir.ActivationFunctionType
ALU = mybir.AluOpType

T = 2048
D = 1024
HALF = 1024
ALPHA = (2.0 * 2) ** 0.25
SLOPES = [float(2.0 ** (-8.0 * (i + 1) / 32)) for i in range(32)]
BIG = 1.0e6


class Tile:
    __slots__ = ("name", "w", "r", "dsem", "dcnt")

    def __init__(self, name):
        self.name = name
        self.w = None
        self.r = {}
        self.dsem = None
        self.dcnt = 0


class Sync:
    def __init__(self, nc, stack):
        self.nc = nc
        self.stack = stack
        self.eng = {"pe": nc.tensor, "act": nc.scalar, "dve": nc.vector, "pool": nc.gpsimd, "sp": nc.sync}
        self.sem = {}
        self.cnt = {}
        self.waited = {k: {} for k in self.eng}
        for k in self.eng:
            self.sem[k] = stack.enter_context(nc.semaphore("s_" + k))
            self.cnt[k] = 0
        self.nsem = 0
        self.nwaits = 0

    def new_sem(self, name):
        self.nsem += 1
        s = self.stack.enter_context(self.nc.semaphore(f"{name}_{self.nsem}"))
        self.sem[s] = s
        return s

    def _wait(self, e, key, val):
        if val <= self.waited[e].get(key, 0):
            return
        self.waited[e][key] = val
        self.eng[e].wait_ge(self.sem[key], val)
        self.nwaits += 1

    def _deps(self, e, reads, writes):
        deps = {}

        def add(d):
            if d is None:
                return
            k, v = d
            if v > deps.get(k, 0):
                deps[k] = v
        for t in reads:
            add(t.w)
        for t in writes:
            add(t.w)
            for k, v in t.r.items():
                add((k, v))
        for k, v in deps.items():
            if k == e and e == "pe":
                continue
            self._wait(e, k, v)

    def op(self, e, fn, reads=(), writes=()):
        self._deps(e, reads, writes)
        ins = fn()
        self.cnt[e] += 1
        v = self.cnt[e]
        ins.then_inc(self.sem[e], 1)
        for t in writes:
            t.w = (e, v)
            t.r = {}
        for t in reads:
            t.r[e] = v
        return ins

    def dma(self, q, out, in_, reads=(), writes=()):
        self._deps(q, reads, writes)
        owner = (list(writes) + list(reads))[0]
        if owner.dsem is None:
            owner.dsem = self.new_sem("d")
        owner.dcnt += 16
        ins = self.eng[q].dma_start(out=out, in_=in_)
        ins.then_inc(owner.dsem, 16)
        for t in writes:
            t.w = (owner.dsem, owner.dcnt)
            t.r = {}
        for t in reads:
            t.r[owner.dsem] = owner.dcnt
        return ins

    def wait_tile_dma(self, e, t):
        if t.dsem is not None:
            self._wait(e, t.dsem, t.dcnt)


def build_program(dbg=False, nlayers=2):
    nc = bass.Bass("TRN2", target_bir_lowering=False)

    def din(name, shape):
        return nc.dram_tensor(name, list(shape), F32, kind="ExternalInput").ap()

    x_d = din("x", [T, D])
    cT_d = din("cT", [128, 8])
    adaw_d = din("ada_w", [2, D, 3 * D])
    adab_d = din("ada_b", [2, 3 * D])
    lng_d = din("ln_g", [2, D])
    lnb_d = din("ln_b", [2, D])
    hgwi_d = din("hg_w_in", [D, 8192])
    hgwo_d = din("hg_w_out", [2048, D])
    lbl_d = din("lbl", [128, 16, 2])
    nw_d = din("nw", [128, 1])
    swwi_d = din("sw_w_in", [D, 4608])
    swwo_d = din("sw_w_out", [2048, D])
    sink_d = din("sinkp", [128, 16])
    ident_d = din("ident", [128, 128])
    tri_d = din("tri", [128, 128])
    smask_d = din("smask", [128, 512])
    dist_d = din("dist", [128, 256])
    out_d = nc.dram_tensor("out", [T, D], F32, kind="ExternalOutput").ap()
    dbg_d = nc.dram_tensor("dbg", [T, D], F32, kind="ExternalOutput").ap() if dbg else None

    with ExitStack() as st:
        S = Sync(nc, st)

        def sb(name, shape, dt=F32):
            return st.enter_context(nc.sbuf_tensor("sb_" + name, list(shape), dt))

        banks = [st.enter_context(nc.psum_tensor(f"bank{i}", [128, 512], F32)) for i in range(8)]
        bt = [Tile(f"bank{i}") for i in range(8)]

        ident_f = sb("ident_f", [128, 128]); t_identf = Tile("ident_f")
        ident_b = sb("ident_b", [128, 128], BF16); t_identb = Tile("ident_b")
        tri_b = sb("tri_b", [128, 128], BF16); t_tri = Tile("tri")
        smask = sb("smask", [128, 512]); t_smask = Tile("smask")
        distm = sb("distm", [128, 256]); t_dist = Tile("dist")
        onesm = sb("onesm", [128, 128], BF16); t_onesm = Tile("onesm")
        oz = sb("oz", [128, 192], BF16); t_oz = Tile("oz")
        cT = sb("cT", [128, 8]); t_cT = Tile("cT")
        cact = sb("cact", [128, 8]); t_cact = Tile("cact")
        cbc = sb("cbc", [128, 8, 128], BF16); t_cbc = Tile("cbc")
        lbl = sb("lbl", [128, 16, 2]); t_lbl = Tile("lbl")
        ldiff = sb("ldiff", [128, 16]); t_ldiff = Tile("ldiff")
        oml = sb("oml", [128, 16]); t_oml = Tile("oml")
        lbm1 = sb("lbm1", [128, 16]); t_lbm1 = Tile("lbm1")
        nw = sb("nw", [128, 1]); t_nw = Tile("nw")
        sinkp = sb("sinkp", [128, 16]); t_sinkp = Tile("sinkp")
        esink = sb("esink", [128, 16]); t_esink = Tile("esink")
        g1p = [sb(f"g1p{L}", [128, D]) for L in range(2)]; t_g1p = [Tile(f"g1p{L}") for L in range(2)]
        modfm = [sb(f"modfm{L}", [128, 2, 8]) for L in range(2)]; t_modfm = [Tile(f"modfm{L}") for L in range(2)]
        S_all = sb("S_all", [128, 16, 128]); t_S = [Tile(f"S{h}") for h in range(16)]
        NSLOT = 6
        slots = [sb(f"slot{i}", [128, 4096], BF16) for i in range(NSLOT)]
        t_slot = [Tile(f"slot{i}") for i in range(NSLOT)]
        hT = sb("hT", [128, 8, HALF], BF16); t_hT = [Tile("hT0"), Tile("hT1")]
        oT = sb("oT", [128, 16, HALF], BF16); t_oT = Tile("oT")
        stC = sb("stC", [128, 2, 6]); t_stC = Tile("stC")
        mvC = sb("mvC", [128, 2]); t_mvC = Tile("mvC")
        smC = sb("smC", [128, 4]); t_smC = Tile("smC")
        Stb = [sb(f"Stb{i}", [128, 128], BF16) for i in range(2)]; t_Stb = [Tile(f"Stb{i}") for i in range(2)]
        ar4 = [sb(f"ar4{i}", [128, 2, 4]) for i in range(2)]; t_ar4 = [Tile(f"ar4{i}") for i in range(2)]
        kT = sb("kT", [128, 4, 128 + HALF], BF16); t_kT = Tile("kT")
        vz = sb("vz", [128, 9, 4, 192], BF16); t_vz = Tile("vz")

        NCH = 15
        arena = sb("arena", [128, NCH * 1024])
        arena_b = sb_alias = None
        T_ch = [Tile(f"ch{i}") for i in range(NCH)]

        def f32v(ch, off, n):
            return arena[:, ch * 1024 + off: ch * 1024 + off + n]

        def bf16v(ch, off, n, nch=1):
            return arena[:, ch * 1024:(ch + nch) * 1024].bitcast(BF16)[:, off:off + n]

        xin = [f32v(11 + i, 0, 1024) for i in range(4)]; t_xin = [T_ch[11 + i] for i in range(4)]
        NXIN = 4
        lnp_g = f32v(8, 0, 1024); lnp_b = f32v(9, 0, 1024)
        tC = f32v(10, 0, 1024); t_tC = T_ch[10]
        X1 = [f32v(i, 0, 1024) for i in range(8)]; t_X1 = [T_ch[i] for i in range(8)]
        v_tm = bf16v(0, 0, 4096, nch=2).rearrange("p (t n) -> p t n", n=512)
        t_vtm = [T_ch[0]] * 4 + [T_ch[1]] * 4
        kp = [f32v(2, 0, 1024), f32v(3, 0, 1024)]; t_kp = [T_ch[2], T_ch[3]]
        szb = [bf16v(4, 0, 1024), bf16v(14, 0, 1024)]; t_szb = [T_ch[4], T_ch[14]]
        sgz = f32v(5, 0, 512); t_sgz = T_ch[5]
        gg = [f32v(6 + 4 * s_, 0, 512) for s_ in range(2)]; t_gg = [T_ch[6 + 4 * s_] for s_ in range(2)]
        bb = [f32v(6 + 4 * s_, 512, 512) for s_ in range(2)]; t_bb = t_gg
        E1 = [f32v(7 + 4 * s_, 0, 512) for s_ in range(2)]; t_E1 = [T_ch[7 + 4 * s_] for s_ in range(2)]
        E2 = [f32v(7 + 4 * s_, 512, 512) for s_ in range(2)]; t_E2 = t_E1
        rs = [f32v(8 + 4 * s_, 0, 512) for s_ in range(2)]; t_rs = [T_ch[8 + 4 * s_] for s_ in range(2)]
        pT = [bf16v(8 + 4 * s_, 1024, 512) for s_ in range(2)]; t_pT = t_rs
        sq = [bf16v(8 + 4 * s_, 1536, 512) for s_ in range(2)]; t_sq = t_rs
        qt = [bf16v(9 + 4 * s_, 0, 512) for s_ in range(2)]; t_qt = [T_ch[9 + 4 * s_] for s_ in range(2)]
        kt = [bf16v(9 + 4 * s_, 512, 512) for s_ in range(2)]; t_kt = t_qt
        kh = [bf16v(9 + 4 * s_, 1024, 512) for s_ in range(2)]; t_kh = t_qt
        khT = [bf16v(9 + 4 * s_, 1536, 512) for s_ in range(2)]; t_khT = t_qt
        wk_rep = bf16v(8, 0, 4096, nch=2).rearrange("p (c g n) -> p c g n", g=4, n=128); t_wkrep = [T_ch[8], T_ch[9]]
        szp = f32v(10, 0, 1024); t_szp = T_ch[10]
        qTp = bf16v(11, 0, 1024); t_qTp = T_ch[11]
        Mh = [bf16v(11, 1024 + a_ * 256, 256) for a_ in range(2)]; t_Mh = [T_ch[11], T_ch[11]]
        pTh = [bf16v(11, 1536 + a_ * 128, 128) for a_ in range(2)]
        pTp = [bf16v(12 + a_, 0, 2048).rearrange("p (k t) -> p k t", t=256) for a_ in range(2)]
        t_pTp = [T_ch[12], T_ch[13]]
        d2 = f32v(14, 0, 512); wdiv = f32v(14, 512, 512); t_d2 = T_ch[14]; t_wdiv = T_ch[14]

        wstate = {"n": 0}

        def wload(src_ap, view):
            i = wstate["n"] % NSLOT
            wstate["n"] += 1
            if view == "in":
                v = slots[i][:, :].rearrange("p (c n) -> p c n", n=512)
            else:
                v = slots[i][:, :].rearrange("p (c n) -> p c n", n=1024)
            S.dma("pool", v, src_ap, writes=[t_slot[i]])
            return v, t_slot[i]

        def w_in_block(w_d, col0):
            return w_d[:, col0:col0 + 512].rearrange("(c p) n -> p c n", p=128)

        def w_out_block(w_d, k):
            return w_d[k * 512:(k + 1) * 512, :].rearrange("(e p) n -> p e n", p=128)

        S.dma("sp", ident_f[:], ident_d, writes=[t_identf])
        S.dma("sp", cT[:], cT_d, writes=[t_cT])
        S.dma("sp", lbl[:], lbl_d, writes=[t_lbl])
        S.dma("sp", nw[:], nw_d, writes=[t_nw])
        S.dma("sp", sinkp[:], sink_d, writes=[t_sinkp])
        S.dma("sp", smask[:], smask_d, writes=[t_smask])
        S.dma("sp", distm[:], dist_d, writes=[t_dist])
        S.dma("pool", ident_b[:], ident_d, writes=[t_identb])
        S.dma("pool", tri_b[:], tri_d, writes=[t_tri])
        S.op("dve", lambda: nc.vector.memset(onesm[:], 1.0 / 128.0), writes=[t_onesm])
        S.op("dve", lambda: nc.vector.memset(oz[:], 0.0), writes=[t_oz])
        S.op("dve", lambda: nc.vector.memset(oz[:, 64:128], 1.0), reads=[t_oz], writes=[t_oz])
        S.op("dve", lambda: nc.vector.memset(S_all[:], 0.0), writes=t_S)
        S.op("dve", lambda: nc.vector.memset(vz[:], 0.0), writes=[t_vz])
        S.op("dve", lambda: nc.vector.memset(kT[:], 0.0), writes=[t_kT])
        S.op("act", lambda: nc.scalar.activation(cact[:], cT[:], AF.Silu), reads=[t_cT], writes=[t_cact])
        S.op("dve", lambda: nc.vector.tensor_copy(cbc[:], cact[:, :].unsqueeze(2).to_broadcast([128, 8, 128])),
             reads=[t_cact], writes=[t_cbc])
        S.op("dve", lambda: nc.vector.tensor_tensor(ldiff[:, :].unsqueeze(2), lbl[:, :, 0:1], lbl[:, :, 1:2], ALU.subtract),
             reads=[t_lbl], writes=[t_ldiff])
        S.op("act", lambda: nc.scalar.activation(oml[:], ldiff[:], AF.Sigmoid, scale=-1.0), reads=[t_ldiff], writes=[t_oml])
        S.op("dve", lambda: nc.vector.tensor_scalar(lbm1[:], oml[:], -1.0, None, ALU.mult), reads=[t_oml], writes=[t_lbm1])
        S.op("act", lambda: nc.scalar.activation(esink[:], sinkp[:], AF.Exp), reads=[t_sinkp], writes=[t_esink])

        modb = tC
        for L in range(nlayers):
            S.dma("sp", g1p[L][:], adab_d[L:L + 1, 2 * D:3 * D].partition_broadcast(128), writes=[t_g1p[L]])
            for part in range(3):
                if part < 2:
                    S.dma("sp", modb, adab_d[L:L + 1, part * D:(part + 1) * D].partition_broadcast(128), writes=[t_tC])
                    dst, t_dst = modb, t_tC
                else:
                    dst, t_dst = g1p[L], t_g1p[L]
                addc = 0.0 if part == 0 else 1.0
                for nb in range(2):
                    wv, t_w = wload(w_in_block(adaw_d[L], part * D + nb * 512), "in")
                    bk = nb
                    for dc in range(8):
                        S.op("pe", lambda dc=dc, wv=wv, bk=bk: nc.tensor.matmul(
                            banks[bk][:], cbc[:, dc, :], wv[:, dc, :], start=(dc == 0), stop=(dc == 7)),
                            reads=[t_cbc, t_w], writes=[bt[bk]])
                    S.op("dve", lambda nb=nb, bk=bk, dst=dst, addc=addc: nc.vector.scalar_tensor_tensor(
                        dst[:, nb * 512:(nb + 1) * 512], banks[bk][:], addc, dst[:, nb * 512:(nb + 1) * 512],
                        ALU.add, ALU.add), reads=[bt[bk], t_dst], writes=[t_dst])
                if part < 2:
                    trv = banks[2][:, 0:256].rearrange("p (c n) -> p c n", n=32)
                    for dc in range(8):
                        S.op("pe", lambda dc=dc, trv=trv: nc.tensor.transpose(
                            trv[:, dc, :], modb[0:32, dc * 128:(dc + 1) * 128], ident_f[0:32, 0:32]),
                            reads=[t_tC, t_identf], writes=[bt[2]])
                    S.op("dve", lambda L=L, part=part, trv=trv: nc.vector.tensor_copy(
                        modfm[L][:, part, :].unsqueeze(2), trv[:, :, 0:1]), reads=[bt[2]], writes=[t_modfm[L]])

        xin_state = {"n": 0}

        def load_x_tile(gt):
            i = xin_state["n"] % NXIN
            xin_state["n"] += 1
            S.dma("sp", xin[i], x_d[gt * 128:(gt + 1) * 128, :], writes=[t_xin[i]])
            return xin[i], t_xin[i]

        def phase_A_ops(L, hf):
            out = [[], []]
            for tb in range(2):
                srcs = []

                def get_srcs(tb=tb, srcs=srcs):
                    if not srcs:
                        for i in range(4):
                            tt = tb * 4 + i
                            if L == 0:
                                srcs.append(load_x_tile(hf * 8 + tt))
                            else:
                                srcs.append((X1[tt], t_X1[tt]))
                    return srcs
                for dc in range(8):
                    def op_(dc=dc, tb=tb, get_srcs=get_srcs):
                        sr = get_srcs()
                        bk = 5 + dc % 2
                        for i in range(4):
                            xa, xt = sr[i]
                            S.op("pe", lambda xa=xa, i=i: nc.tensor.transpose(
                                banks[bk][:, i * 128:(i + 1) * 128], xa[:, dc * 128:(dc + 1) * 128], ident_f[:]),
                                reads=[xt, t_identf], writes=[bt[bk]])
                        S.op("act", lambda: nc.scalar.activation(
                            hT[:, dc, tb * 512:(tb + 1) * 512], banks[bk][:], AF.Identity,
                            bias=modfm[L][:, 0, dc:dc + 1], scale=modfm[L][:, 1, dc:dc + 1]),
                            reads=[bt[bk], t_modfm[L]], writes=[t_hT[tb]])
                    out[tb].append(op_)
            return out

        def phase_A(L, hf):
            for lst in phase_A_ops(L, hf):
                for f_ in lst:
                    f_()

        def phase_C(L, hf, w_d, fillA=None):
            S.dma("sp", lnp_g, lng_d[L:L + 1, :].partition_broadcast(128), writes=[T_ch[8]])
            S.dma("sp", lnp_b, lnb_d[L:L + 1, :].partition_broadcast(128), writes=[T_ch[9]])
            wo = [WO_PRE.pop((L, hf, k)) if (L, hf, k) in WO_PRE else wload(w_out_block(w_d, k), "out") for k in range(4)]
            last = (L == nlayers - 1)
            pend_fin = []
            pend_mid = []
            for tt in range(8):
                gt = hf * 8 + tt
                if L == 0:
                    xa, xt = load_x_tile(gt)
                else:
                    xa, xt = X1[tt], t_X1[tt]
                for nb in range(2):
                    bk = (tt % 2) * 2 + nb
                    for e in range(16):
                        wv, t_w = wo[e // 4]
                        S.op("pe", lambda e=e, wv=wv, bk=bk, nb=nb, tt=tt: nc.tensor.matmul(
                            banks[bk][:], oT[:, e, tt * 128:(tt + 1) * 128], wv[:, e % 4, nb * 512:(nb + 1) * 512],
                            start=(e == 0), stop=(e == 15)), reads=[t_oT, t_w], writes=[bt[bk]])
                    S.op("dve", lambda bk=bk, nb=nb: nc.vector.tensor_tensor(
                        tC[:, nb * 512:(nb + 1) * 512], banks[bk][:], g1p[L][:, nb * 512:(nb + 1) * 512], ALU.mult),
                        reads=[bt[bk], t_g1p[L]], writes=[t_tC])
                S.op("dve", lambda xa=xa: nc.vector.scalar_tensor_tensor(xa, xa, ALPHA, tC, ALU.mult, ALU.add),
                     reads=[xt, t_tC], writes=[xt])
                while pend_mid:
                    pend_mid.pop(0)()
                for nb in range(2):
                    S.op("dve", lambda nb=nb, xa=xa: nc.vector.bn_stats(stC[:, nb, :], xa[:, nb * 512:(nb + 1) * 512]),
                         reads=[xt], writes=[t_stC])
                S.op("dve", lambda: nc.vector.bn_aggr(mvC[:], stC[:, :, :].rearrange("p a b -> p (a b)")),
                     reads=[t_stC], writes=[t_mvC])
                while pend_fin:
                    pend_fin.pop(0)()
                S.op("act", lambda: nc.scalar.activation(smC[:, 0:1], mvC[:, 1:2], AF.Ln, bias=1e-5, scale=1.0),
                     reads=[t_mvC], writes=[t_smC])
                S.op("act", lambda: nc.scalar.activation(smC[:, 1:2], smC[:, 0:1], AF.Exp, scale=-0.5),
                     reads=[t_smC], writes=[t_smC])
                def mid(xa=xa, xt=xt):
                    S.op("dve", lambda: nc.vector.tensor_scalar(smC[:, 2:3], mvC[:, 0:1], smC[:, 1:2], -1.0, ALU.mult, ALU.mult),
                         reads=[t_mvC, t_smC], writes=[t_smC])
                    S.op("act", lambda: nc.scalar.activation(xa, xa, AF.Identity, bias=smC[:, 2:3], scale=smC[:, 1:2]),
                         reads=[xt, t_smC], writes=[xt])
                pend_mid.append(mid)
                def fin(xa=xa, xt=xt, tt=tt, gt=gt):
                    S.op("dve", lambda: nc.vector.tensor_tensor(xa, xa, lnp_g, ALU.mult),
                         reads=[xt, T_ch[8]], writes=[xt])
                    if not last:
                        S.op("dve", lambda: nc.vector.tensor_tensor(X1[tt], xa, lnp_b, ALU.add),
                             reads=[xt, T_ch[9]], writes=[t_X1[tt]])
                        if dbg:
                            S.dma("sp", dbg_d[gt * 128:(gt + 1) * 128, :], X1[tt], reads=[t_X1[tt]])
                    else:
                        S.op("dve", lambda: nc.vector.tensor_tensor(xa, xa, lnp_b, ALU.add),
                             reads=[xt, T_ch[9]], writes=[xt])
                        S.dma("sp", out_d[gt * 128:(gt + 1) * 128, :], xa, reads=[xt])
                        out_tiles.add(xt)
                pend_fin.append(fin)
                if tt == 7:
                    while pend_mid:
                        pend_mid.pop(0)()
                    while pend_fin:
                        pend_fin.pop(0)()
                if fillA is not None:
                    if L == 0:
                        if tt >= 4:
                            for _ in range(2):
                                if fillA[0]:
                                    fillA[0].pop(0)()
                        if tt == 7:
                            for lst in fillA:
                                while lst:
                                    lst.pop(0)()
                    else:
                        lst = fillA[0] if fillA[0] else fillA[1]
                        for _ in range(2):
                            if lst:
                                lst.pop(0)()
            while pend_mid:
                pend_mid.pop(0)()
            while pend_fin:
                pend_fin.pop(0)()

        out_tiles = set()
        WO_PRE = {}

        def layer0_mixer(hf):
            from collections import deque
            FQ = deque()
            st_ = {"enq": 0, "pop": 0}
            marks = {}

            def enq(fn):
                FQ.append(fn)
                st_["enq"] += 1

            def run(n, limit):
                while n > 0 and FQ and st_["pop"] < limit:
                    FQ.popleft()()
                    st_["pop"] += 1
                    n -= 1

            def flush_to(mark):
                while st_["pop"] < mark:
                    FQ.popleft()()
                    st_["pop"] += 1
                    tick()

            DQ = []

            def defer(fn, k):
                DQ.append([k, fn])

            def tick():
                try_loads()
                due = [d for d in DQ if d[0] <= 1]
                rest = [d for d in DQ if d[0] > 1]
                DQ[:] = rest
                for d in rest:
                    d[0] -= 1
                for d in due:
                    d[1]()

            def drain_deferred():
                while DQ:
                    tick()

            t_vtm = [Tile(f"vtm{i_}") for i_ in range(8)]
            t_kp = [[Tile(f"kp{i_}{j_}") for j_ in range(2)] for i_ in range(2)]
            t_szb = [[Tile(f"szb{i_}{j_}") for j_ in range(2)] for i_ in range(2)]
            t_sgz = Tile("sgz")
            mk2 = lambda n_: [Tile(f"{n_}{i_}") for i_ in range(2)]
            t_gg, t_bb, t_E1, t_E2 = mk2("gg"), mk2("bb"), mk2("E1"), mk2("E2")
            t_rs, t_pT, t_sq = mk2("rs"), mk2("pT"), mk2("sq")
            t_qt, t_kt, t_kh, t_khT = mk2("qt"), mk2("kt"), mk2("kh"), mk2("khT")
            fine = {0: t_vtm[0:4], 1: t_vtm[4:8], 2: t_kp[0], 3: t_kp[1], 4: t_szb[0], 5: [t_sgz], 14: t_szb[1]}
            for s_ in range(2):
                fine[6 + 4 * s_] = [t_gg[s_], t_bb[s_]]
                fine[7 + 4 * s_] = [t_E1[s_], t_E2[s_]]
                fine[8 + 4 * s_] = [t_rs[s_], t_pT[s_], t_sq[s_]]
                fine[9 + 4 * s_] = [t_qt[s_], t_kt[s_], t_kh[s_], t_khT[s_]]
            for ch_, tl_ in fine.items():
                S.op("dve", lambda ch_=ch_: nc.vector.memset(f32v(ch_, 0, 1), 0.0), writes=[T_ch[ch_]] + tl_)
            for bi_ in range(2):
                S.op("dve", lambda bi_=bi_: nc.vector.memset(pT[bi_], 0.0), writes=[t_pT[bi_]])

            W = {}
            WOFF = {"f": 2048, "z": 6144, "q": 0, "v": 4096}
            load_order = [("v", 0), ("f", 0), ("z", 0), ("q", 0)]
            for g_ in range(1, 4):
                load_order += [("f", g_), ("z", g_), ("q", g_), ("v", g_)]
            load_order += [("wo", k_) for k_ in range(4)]
            uses = {k_: 0 for k_ in load_order}
            lstate = {"next": 0}

            def try_loads():
                while lstate["next"] < len(load_order):
                    i = lstate["next"]
                    key = load_order[i]
                    if i >= NSLOT and uses[load_order[i - NSLOT]] < 64:
                        break
                    if key[0] == "wo":
                        WO_PRE[(0, hf, key[1])] = wload(w_out_block(hgwo_d, key[1]), "out")
                    else:
                        W[key] = wload(w_in_block(hgwi_d, WOFF[key[0]] + key[1] * 512), "in")
                    lstate["next"] += 1

            def load_w(kind, grp):
                assert (kind, grp) in W, (kind, grp)
                uses[(kind, grp)] += 1
                return W[(kind, grp)]

            def V(grp):
                try_loads()
                wvv, t_wv = load_w("v", grp)
                uses[("v", grp)] += 63
                for tt in range(8):
                    bk = tt % 2
                    for dc in range(8):
                        S.op("pe", lambda dc=dc, bk=bk, tt=tt: nc.tensor.matmul(
                            banks[bk][:], hT[:, dc, tt * 128:(tt + 1) * 128], wvv[:, dc, :],
                            start=(dc == 0), stop=(dc == 7)), reads=[t_hT[tt // 4], t_wv], writes=[bt[bk]])
                    if tt % 2 == 0:
                        S.op("act", lambda bk=bk, tt=tt: nc.scalar.copy(v_tm[:, tt, :], banks[bk][:]),
                             reads=[bt[bk]], writes=[t_vtm[tt]])
                    else:
                        S.op("dve", lambda bk=bk, tt=tt: nc.vector.tensor_copy(v_tm[:, tt, :], banks[bk][:]),
                             reads=[bt[bk]], writes=[t_vtm[tt]])

            def enq_A(head, blk):
                grp, hh = divmod(head, 4)
                cs0 = hh * 128
                hp = head % 2
                if True:
                    tsl = slice(blk * 512, (blk + 1) * 512)
                    for dc in range(8):
                        def f_(dc=dc, tsl=tsl, blk=blk):
                            wf, t_wf = load_w("f", grp)
                            S.op("pe", lambda: nc.tensor.matmul(
                                banks[0][:], wf[:, dc, cs0:cs0 + 128], hT[:, dc, tsl], start=(dc == 0), stop=(dc == 7)),
                                reads=[t_hT[blk], t_wf], writes=[bt[0]])
                            if dc == 7:
                                S.op("act", lambda: nc.scalar.activation(kp[hp][:, tsl], banks[0][:], AF.Exp),
                                     reads=[bt[0]], writes=[t_kp[hp][blk]])
                                S.op("act", lambda: nc.scalar.activation(kp[hp][:, tsl], kp[hp][:, tsl], AF.Ln, bias=1.0, scale=1.0),
                                     reads=[t_kp[hp][blk]], writes=[t_kp[hp][blk]])
                                S.op("act", lambda: nc.scalar.activation(kp[hp][:, tsl], kp[hp][:, tsl], AF.Exp, scale=-1.0),
                                     reads=[t_kp[hp][blk]], writes=[t_kp[hp][blk]])
                        enq(f_)
                    for dc in range(8):
                        def z_(dc=dc, tsl=tsl, blk=blk):
                            wz, t_wz = load_w("z", grp)
                            S.op("pe", lambda: nc.tensor.matmul(
                                banks[1][:], wz[:, dc, cs0:cs0 + 128], hT[:, dc, tsl], start=(dc == 0), stop=(dc == 7)),
                                reads=[t_hT[blk], t_wz], writes=[bt[1]])
                            if dc == 7:
                                S.op("act", lambda: nc.scalar.activation(sgz, banks[1][:], AF.Exp, scale=-1.0),
                                     reads=[bt[1]], writes=[t_sgz])
                                S.op("act", lambda: nc.scalar.activation(sgz, sgz, AF.Ln, bias=1.0, scale=1.0),
                                     reads=[t_sgz], writes=[t_sgz])
                                S.op("act", lambda: nc.scalar.activation(sgz, sgz, AF.Exp, scale=-1.0),
                                     reads=[t_sgz], writes=[t_sgz])
                                defer(lambda: S.op("dve", lambda: nc.vector.tensor_tensor(
                                    szb[hp][:, tsl], banks[1][:], sgz, ALU.mult),
                                    reads=[bt[1], t_sgz], writes=[t_szb[hp][blk]]), 3)
                        enq(z_)
                marks[("A", head, blk)] = st_["enq"]

            def gating(head, blk, piece):
                hp = head % 2
                bi = blk
                qb = 2
                tsl = slice(blk * 512, (blk + 1) * 512)
                b3 = bb[bi].rearrange("p (c t) -> p c t", t=128)
                g3 = gg[bi].rearrange("p (c t) -> p c t", t=128)
                if piece == 0:
                    S.op("act", lambda: nc.scalar.activation(
                        gg[bi], kp[hp][:, tsl], AF.Ln, bias=1.0, scale=lbm1[:, head:head + 1]),
                        reads=[t_kp[hp][blk], t_lbm1], writes=[t_gg[bi]])
                elif piece == 1:
                    S.op("dve", lambda: nc.vector.tensor_tensor_scan(bb[bi], smask[:], gg[bi], 0.0, ALU.mult, ALU.add),
                         reads=[t_smask, t_gg[bi]], writes=[t_bb[bi]])
                elif piece == 2:
                    S.op("act", lambda: nc.scalar.activation(ar4[bi][:, 0, :].unsqueeze(2), b3[:, :, 127:128], AF.Exp),
                         reads=[t_bb[bi]], writes=[t_ar4[bi]])
                    S.op("act", lambda: nc.scalar.activation(ar4[bi][:, 1, :].unsqueeze(2), b3[:, :, 63:64], AF.Exp),
                         reads=[t_bb[bi]], writes=[t_ar4[bi]])
                    S.op("pool", lambda: nc.gpsimd.tensor_tensor(
                        g3, b3, b3[:, :, 63:64].to_broadcast([128, 4, 128]), ALU.subtract),
                        reads=[t_bb[bi]], writes=[t_gg[bi]])
                elif piece == 3:
                    S.op("act", lambda: nc.scalar.activation(E1[bi], gg[bi], AF.Exp), reads=[t_gg[bi]], writes=[t_E1[bi]])
                    S.op("act", lambda: nc.scalar.activation(E2[bi], gg[bi], AF.Exp, scale=-1.0),
                         reads=[t_gg[bi]], writes=[t_E2[bi]])
                elif piece == 4:
                    S.op("dve", lambda: nc.vector.scalar_tensor_tensor(
                        qt[bi], banks[qb][:], oml[:, head:head + 1], E1[bi], ALU.mult, ALU.mult),
                        reads=[bt[qb], t_oml, t_E1[bi]], writes=[t_qt[bi]])
                else:
                    S.op("pool", lambda: nc.gpsimd.tensor_tensor(kt[bi], kp[hp][:, tsl], E2[bi], ALU.mult),
                         reads=[t_kp[hp][blk], t_E2[bi]], writes=[t_kt[bi]])
                    e13 = E1[bi].rearrange("p (c t) -> p c t", t=128)
                    S.op("pool", lambda: nc.gpsimd.tensor_tensor(
                        kh[bi].rearrange("p (c t) -> p c t", t=128), kt[bi].rearrange("p (c t) -> p c t", t=128),
                        e13[:, :, 127:128].to_broadcast([128, 4, 128]), ALU.mult),
                        reads=[t_kt[bi], t_E1[bi]], writes=[t_kh[bi]])

            def enq_B(head, blk):
                grp, hh = divmod(head, 4)
                cs0 = hh * 128
                qb = 2
                tsl = slice(blk * 512, (blk + 1) * 512)
                marks[("Bs", head, blk)] = st_["enq"]
                for dc in range(8):
                    def q_(dc=dc):
                        wq, t_wq = load_w("q", grp)
                        S.op("pe", lambda: nc.tensor.matmul(
                            banks[qb][:], wq[:, dc, cs0:cs0 + 128], hT[:, dc, tsl], start=(dc == 0), stop=(dc == 7)),
                            reads=[t_hT[blk], t_wq], writes=[bt[qb]])
                    enq(q_)
                marks[("B", head, blk)] = st_["enq"]

            def C_head(head, blk, limit, G):
                bi = blk
                for c in range(4):
                    c0 = c * 128
                    S.op("pe", lambda c0=c0: nc.tensor.matmul(
                        banks[5][:, c0 + 64:c0 + 128], kt[bi][:, c0:c0 + 128], qt[bi][:, c0 + 64:c0 + 128],
                        start=True, stop=True), reads=[t_kt[bi], t_qt[bi]], writes=[bt[5]])
                    S.op("pe", lambda c0=c0: nc.tensor.matmul(
                        banks[5][0:64, c0:c0 + 64], kt[bi][:, c0:c0 + 64], qt[bi][:, c0:c0 + 64],
                        start=True, stop=True), reads=[t_kt[bi], t_qt[bi]], writes=[bt[5]])
                trk = banks[4][:, 0:256].bitcast(BF16)
                for c in range(4):
                    S.op("pe", lambda c=c: nc.tensor.transpose(
                        trk[:, c * 128:(c + 1) * 128], kh[bi][:, c * 128:(c + 1) * 128], ident_b[:]),
                        reads=[t_kh[bi], t_identb], writes=[bt[4]])
                p3 = pT[bi].rearrange("p (c t) -> p c t", t=128)
                s3 = banks[5][:, :].rearrange("p (c t) -> p c t", t=128)
                S.op("dve", lambda: nc.vector.tensor_tensor(
                    p3[:, :, 64:128], s3[:, :, 64:128],
                    tri_b[:, 64:128].unsqueeze(1).to_broadcast([128, 4, 64]), ALU.mult),
                    reads=[bt[5], t_tri], writes=[t_pT[bi]])
                S.op("dve", lambda: nc.vector.tensor_tensor(
                    p3[0:64, :, 0:64], s3[0:64, :, 0:64],
                    tri_b[0:64, 0:64].unsqueeze(1).to_broadcast([64, 4, 64]), ALU.mult),
                    reads=[bt[5], t_tri], writes=[t_pT[bi]])
                S.op("dve", lambda: nc.vector.tensor_copy(khT[bi], trk), reads=[bt[4]], writes=[t_khT[bi]])

            def C_steps(head, blk, limit, G):
                grp, hh = divmod(head, 4)
                cs0 = hh * 128
                bi = blk
                ob = 6 if blk == 0 else 3
                G("n2")
                tick()
                run(4, limit)
                for c in range(4):
                    first = (hf == 0 and blk == 0 and c == 0)
                    csl = slice(c * 128, (c + 1) * 128)
                    vt = blk * 4 + c
                    si = c % 2
                    if not first:
                        S.op("dve", lambda si=si, c=c: nc.vector.tensor_scalar(
                            Stb[si][:], S_all[:, head, :], ar4[bi][:, 1, c:c + 1], None, ALU.mult),
                            reads=[t_S[head], t_ar4[bi]], writes=[t_Stb[si]])
                    S.op("pe", lambda csl=csl, vt=vt, first=first: nc.tensor.matmul(
                        banks[ob][:, csl], v_tm[:, vt, cs0:cs0 + 128], pT[bi][:, csl], start=True, stop=first),
                        reads=[t_vtm[vt], t_pT[bi]], writes=[bt[ob]])
                    S.op("pe", lambda csl=csl, vt=vt: nc.tensor.matmul(
                        banks[4][:, 256:384], khT[bi][:, csl], v_tm[:, vt, cs0:cs0 + 128], start=True, stop=True),
                        reads=[t_khT[bi], t_vtm[vt]], writes=[bt[4]])
                    if not first:
                        S.op("pe", lambda csl=csl, si=si: nc.tensor.matmul(
                            banks[ob][:, csl], Stb[si][:], qt[bi][:, csl], start=False, stop=True),
                            reads=[t_Stb[si], t_qt[bi]], writes=[bt[ob]])
                    if first:
                        S.op("dve", lambda: nc.vector.tensor_copy(S_all[:, head, :], banks[4][:, 256:384]),
                             reads=[bt[4]], writes=[t_S[head]])
                    else:
                        S.op("dve", lambda c=c: nc.vector.scalar_tensor_tensor(
                            S_all[:, head, :], S_all[:, head, :], ar4[bi][:, 0, c:c + 1], banks[4][:, 256:384],
                            ALU.mult, ALU.add), reads=[t_S[head], t_ar4[bi], bt[4]], writes=[t_S[head]])
                    tick()
                    run(5, limit)
                    for p_ in (("n3",), ("n5",), ("nn0",), ("n4", "nn1"))[c]:
                        G(p_)

            def C_tail(head, blk):
                hp = head % 2
                bi = blk
                ob = 6 if blk == 0 else 3
                tsl = slice(blk * 512, (blk + 1) * 512)
                S.op("act", lambda: nc.scalar.activation(sq[bi], banks[ob][:], AF.Square), reads=[bt[ob]], writes=[t_sq[bi]])

                def st1():
                    S.op("pe", lambda: nc.tensor.matmul(banks[7][:], onesm[:], sq[bi], start=True, stop=True),
                         reads=[t_onesm, t_sq[bi]], writes=[bt[7]])

                def st2():
                    S.op("act", lambda: nc.scalar.activation(rs[bi], banks[7][:], AF.Ln, bias=1e-6, scale=1.0),
                         reads=[bt[7]], writes=[t_rs[bi]])
                    S.op("act", lambda: nc.scalar.activation(rs[bi], rs[bi], AF.Exp, scale=-0.5),
                         reads=[t_rs[bi]], writes=[t_rs[bi]])

                def st3():
                    S.op("dve", lambda: nc.vector.scalar_tensor_tensor(
                        rs[bi], banks[ob][:], nw[:, 0:1], rs[bi], ALU.mult, ALU.mult),
                        reads=[bt[ob], t_rs[bi], t_nw], writes=[t_rs[bi]])
                    S.op("pool", lambda: nc.gpsimd.tensor_tensor(oT[:, head, tsl], rs[bi], szb[hp][:, tsl], ALU.mult),
                         reads=[t_rs[bi], t_szb[hp][blk]], writes=[t_oT])
                defer(st1, 2)
                defer(st2, 3)
                defer(st3, 5)

            V(0)
            enq_A(0, 0)
            enq_B(0, 0)
            enq_A(0, 1)
            flush_to(st_["enq"])
            drain_deferred()
            for piece in range(6):
                gating(0, 0, piece)
            gating(0, 1, 0)
            gating(0, 1, 1)
            units = [(h, b_) for h in range(16) for b_ in range(2)]
            C_head(0, 0, 0, None)
            for ui, (head, blk) in enumerate(units):
                nh = head + 1
                if blk == 0:
                    if nh < 16:
                        enq_A(nh, 0)
                    enq_B(head, 1)
                else:
                    if nh < 16:
                        enq_A(nh, 1)
                        enq_B(nh, 0)
                nu = units[ui + 1] if ui + 1 < len(units) else None
                nnu = units[ui + 2] if ui + 2 < len(units) else None
                limit = st_["enq"]

                def G(tag, nu=nu, nnu=nnu):
                    u_ = nnu if tag.startswith("nn") else nu
                    if u_ is not None:
                        gating(u_[0], u_[1], int(tag[-1]))
                C_steps(head, blk, limit, G)
                flush_to(st_["enq"])
                new_group = (blk == 1 and nh % 4 == 0 and nh < 16)
                if new_group:
                    C_tail(head, blk)
                    drain_deferred()
                    V(nh // 4)
                    C_head(nh, 0, 0, None)
                else:
                    if nu is not None:
                        C_head(nu[0], nu[1], 0, None)
                    C_tail(head, blk)
            flush_to(st_["enq"])
            drain_deferred()
            for ch_, tl_ in fine.items():
                S.op("dve", lambda ch_=ch_: nc.vector.memset(f32v(ch_, 0, 1), 0.0), reads=tl_, writes=[T_ch[ch_]] + tl_)

        def layer1_mixer(hf):
            wkv, t_wkv = wload(w_in_block(swwi_d, 2048), "in")
            for half in range(2):
                S.op("dve", lambda half=half: nc.vector.tensor_copy(
                    wk_rep[:, :, :, half * 64:(half + 1) * 64],
                    wkv[:, :, 0:256].rearrange("p c (g d) -> p c g d", d=64)),
                    reads=[t_wkv], writes=t_wkrep)
            if hf == 1:
                S.op("dve", lambda: nc.vector.tensor_copy(kT[:, :, 0:128], kT[:, :, HALF:HALF + 128]),
                     reads=[t_kT], writes=[t_kT])
                S.op("dve", lambda: nc.vector.tensor_copy(vz[:, 0, :, :], vz[:, 8, :, :]), reads=[t_vz], writes=[t_vz])
            for g in range(4):
                for tb in range(2):
                    bk = tb
                    for dc in range(8):
                        S.op("pe", lambda dc=dc, g=g, tb=tb, bk=bk: nc.tensor.matmul(
                            banks[bk][:], wk_rep[:, dc, g, :], hT[:, dc, tb * 512:(tb + 1) * 512],
                            start=(dc == 0), stop=(dc == 7)), reads=t_wkrep + [t_hT[tb]], writes=[bt[bk]])
                    S.op("act", lambda g=g, tb=tb, bk=bk: nc.scalar.copy(
                        kT[:, g, 128 + tb * 512:128 + (tb + 1) * 512], banks[bk][:]), reads=[bt[bk]], writes=[t_kT])
            for tt in range(8):
                bk = 2 + tt % 2
                for dc in range(8):
                    S.op("pe", lambda dc=dc, tt=tt, bk=bk: nc.tensor.matmul(
                        banks[bk][:, 0:256], hT[:, dc, tt * 128:(tt + 1) * 128], wkv[:, dc, 256:512],
                        start=(dc == 0), stop=(dc == 7)), reads=[t_hT[tt // 4], t_wkv], writes=[bt[bk]])
                S.op("act", lambda tt=tt, bk=bk: nc.scalar.copy(
                    vz[:, 1 + tt, :, 64:128], banks[bk][:, 0:256].rearrange("p (g d) -> p g d", d=64)),
                    reads=[bt[bk]], writes=[t_vz])
            kb_lo = -1 if hf == 1 else 0
            scbank = {"n": 0}
            szp2 = [szp, f32v(8, 0, 1024)]
            t_szp2 = [[Tile(f"szp{i_}{k_}") for k_ in range(2)] for i_ in range(2)]
            szp_ch = [T_ch[10], T_ch[8]]
            qTp2 = [qTp, bf16v(9, 0, 1024)]; t_qTp2 = [T_ch[11], T_ch[9]]
            WQ = {}

            ft = [[Tile(f"pTp{a_}_{k_}") for k_ in range(5)] for a_ in range(2)]
            t_rec = [Tile("rec0"), Tile("rec1")]
            recb = [wdiv, d2]
            for a_ in range(2):
                S.op("dve", lambda a_=a_: nc.vector.memset(pTp[a_][:, 0, 0:2], 0.0), writes=[t_pTp[a_], T_ch[11]] + ft[a_])
            S.op("dve", lambda: nc.vector.memset(d2[:, 0:1], 0.0), writes=[T_ch[14]] + t_rec)
            for i_ in range(2):
                S.op("dve", lambda i_=i_: nc.vector.memset(szp2[i_][:, 0:1], 0.0), writes=[szp_ch[i_]] + t_szp2[i_])
            sgq = d2

            def proj_ops(j):
                ops = []
                if j % 4 == 0:
                    WQ["q"] = wload(w_in_block(swwi_d, (j // 4) * 512), "in")
                    WQ["z"] = wload(w_in_block(swwi_d, 2560 + (j // 4) * 512), "in")
                wq, t_wq = WQ["q"]
                wz, t_wz = WQ["z"]
                cs0 = (j % 4) * 128
                pj = j % 2
                for tb in range(2):
                    bk = tb
                    dst = szp2[pj][:, tb * 512:(tb + 1) * 512]
                    for dc in range(8):
                        def z_(dc=dc, tb=tb, bk=bk, dst=dst):
                            S.op("pe", lambda: nc.tensor.matmul(
                                banks[bk][:], wz[:, dc, cs0:cs0 + 128], hT[:, dc, tb * 512:(tb + 1) * 512],
                                start=(dc == 0), stop=(dc == 7)), reads=[t_wz, t_hT[tb]], writes=[bt[bk]])
                            if dc == 7:
                                S.op("act", lambda: nc.scalar.activation(dst, banks[bk][:], AF.Exp, scale=-1.0),
                                     reads=[bt[bk]], writes=[t_szp2[pj][tb]])
                                S.op("act", lambda: nc.scalar.activation(dst, dst, AF.Ln, bias=1.0, scale=1.0),
                                     reads=[t_szp2[pj][tb]], writes=[t_szp2[pj][tb]])
                                S.op("act", lambda: nc.scalar.activation(dst, dst, AF.Exp, scale=-1.0),
                                     reads=[t_szp2[pj][tb]], writes=[t_szp2[pj][tb]])
                                S.op("dve", lambda: nc.vector.tensor_tensor(dst, banks[bk][:], dst, ALU.mult),
                                     reads=[bt[bk], t_szp2[pj][tb]], writes=[t_szp2[pj][tb]])
                        ops.append(z_)
                for tb in range(2):
                    bk = tb
                    for dc in range(8):
                        def q_(dc=dc, tb=tb, bk=bk):
                            S.op("pe", lambda: nc.tensor.matmul(
                                banks[bk][:], wq[:, dc, cs0:cs0 + 128], hT[:, dc, tb * 512:(tb + 1) * 512],
                                start=(dc == 0), stop=(dc == 7)), reads=[t_wq, t_hT[tb]], writes=[bt[bk]])
                            if dc == 7:
                                S.op("dve", lambda: nc.vector.tensor_copy(qTp2[pj][:, tb * 512:(tb + 1) * 512], banks[bk][:]),
                                     reads=[bt[bk]], writes=[t_qTp2[pj]])
                        ops.append(q_)
                return ops

            def scores(j, fill):
                g = j // 4
                pj = j % 2
                qT_, t_qT_ = qTp2[pj], t_qTp2[pj]

                def pop(n):
                    for _ in range(n):
                        if fill:
                            fill.pop(0)()
                for a in range(2):
                    h = 2 * j + a
                    S.op("act", lambda a=a, h=h: nc.scalar.activation(Mh[a], distm[:], AF.Exp, scale=-SLOPES[h]),
                         reads=[t_dist], writes=[t_Mh[a]])
                if kb_lo < 0:
                    for a in range(2):
                        pb = a * 64
                        bk = 2 + a
                        S.op("pe", lambda bk=bk, pb=pb: nc.tensor.matmul(
                            banks[bk][:, 0:128], kT[pb:pb + 64, g, 0:128], qT_[pb:pb + 64, 0:128], start=True, stop=True),
                            reads=[t_kT, t_qT_], writes=[bt[bk]])
                    for a in range(2):
                        bk = 2 + a
                        S.op("act", lambda bk=bk, a=a: nc.scalar.activation(pTh[a], banks[bk][:, 0:128], AF.Exp, scale=0.125),
                             reads=[bt[bk]], writes=[ft[a][4]])
                        S.op("dve", lambda a=a: nc.vector.tensor_tensor(pTh[a], pTh[a], Mh[a][:, 128:256], ALU.mult),
                             reads=[ft[a][4], t_Mh[a]], writes=[ft[a][4]])
                for k0 in range(0, 8, 2):
                    for ii in range(2):
                        for a in range(2):
                            pb = a * 64
                            bk = 2 + a
                            kb = k0 + ii
                            q1 = min(kb + 2, 8)
                            ncol = (q1 - kb) * 128
                            S.op("pe", lambda ii=ii, kb=kb, q1=q1, ncol=ncol, bk=bk, pb=pb: nc.tensor.matmul(
                                banks[bk][:, ii * 256: ii * 256 + ncol],
                                kT[pb:pb + 64, g, (kb + 1) * 128:(kb + 2) * 128],
                                qT_[pb:pb + 64, kb * 128:q1 * 128], start=True, stop=True),
                                reads=[t_kT, t_qT_], writes=[bt[bk]])
                    for a in range(2):
                        bk = 2 + a
                        tk = ft[a][k0 // 2]
                        if k0 < 6:
                            S.op("act", lambda bk=bk, a=a, k0=k0, tk=tk: nc.scalar.activation(
                                pTp[a][:, k0:k0 + 2, :], banks[bk][:, :].rearrange("p (k t) -> p k t", t=256),
                                AF.Exp, scale=0.125), reads=[bt[bk]], writes=[tk])
                            S.op("dve", lambda a=a, k0=k0, tk=tk: nc.vector.tensor_tensor(
                                pTp[a][:, k0:k0 + 2, :], pTp[a][:, k0:k0 + 2, :],
                                Mh[a].unsqueeze(1).to_broadcast([128, 2, 256]), ALU.mult),
                                reads=[tk, t_Mh[a]], writes=[tk])
                        else:
                            for ii, wd in ((0, 256), (1, 128)):
                                S.op("act", lambda bk=bk, a=a, ii=ii, wd=wd, tk=tk: nc.scalar.activation(
                                    pTp[a][:, 6 + ii, 0:wd], banks[bk][:, ii * 256:ii * 256 + wd],
                                    AF.Exp, scale=0.125), reads=[bt[bk]], writes=[tk])
                                S.op("dve", lambda a=a, ii=ii, wd=wd, tk=tk: nc.vector.tensor_tensor(
                                    pTp[a][:, 6 + ii, 0:wd], pTp[a][:, 6 + ii, 0:wd], Mh[a][:, 0:wd], ALU.mult),
                                    reads=[tk, t_Mh[a]], writes=[tk])
                    pop(8)

            def pv(j):
                g = j // 4
                pj = j % 2
                for tb in range(2):
                    for which in range(2):
                        bk = 4 + 2 * tb + which
                        for qi in range(4):
                            n = tb * 4 + qi
                            csl = slice(qi * 128, (qi + 1) * 128)
                            contrib = [(kb, a) for kb in (n - 1, n) if kb >= kb_lo for a in range(2)]
                            for ci, (kb, a) in enumerate(contrib):
                                if which == 0:
                                    lhsT = vz[:, kb + 1, g, 64:192] if a == 0 else vz[:, kb + 1, g, 0:128]
                                else:
                                    lhsT = oz[:, 64:192] if a == 0 else oz[:, 0:128]
                                if kb < 0:
                                    rhs = pTh[a]
                                    tk = ft[a][4]
                                elif kb == n:
                                    rhs = pTp[a][:, kb, 0:128]
                                    tk = ft[a][kb // 2]
                                else:
                                    rhs = pTp[a][:, kb, 128:256]
                                    tk = ft[a][kb // 2]
                                S.op("pe", lambda lhsT=lhsT, rhs=rhs, bk=bk, csl=csl, ci=ci, nct=len(contrib): nc.tensor.matmul(
                                    banks[bk][:, csl], lhsT, rhs, start=(ci == 0), stop=(ci == nct - 1)),
                                    reads=[t_vz, t_oz, tk], writes=[bt[bk]])
                for tb in range(2):
                    nb_, db_ = 4 + 2 * tb, 5 + 2 * tb
                    tsl = slice(tb * 512, (tb + 1) * 512)
                    rb, t_rb = recb[tb], t_rec[tb]
                    S.op("act", lambda db_=db_, rb=rb: nc.scalar.activation(rb, banks[db_][:], AF.Ln, bias=esink[:, j:j + 1], scale=1.0),
                         reads=[bt[db_], t_esink], writes=[t_rb])
                    S.op("act", lambda rb=rb: nc.scalar.activation(rb, rb, AF.Exp, scale=-1.0), reads=[t_rb], writes=[t_rb])
                    S.op("dve", lambda tsl=tsl, rb=rb: nc.vector.tensor_tensor(rb, szp2[pj][:, tsl], rb, ALU.mult),
                         reads=[t_szp2[pj][tb], t_rb], writes=[t_rb])
                    S.op("dve", lambda tsl=tsl, nb_=nb_, rb=rb: nc.vector.tensor_tensor(oT[:, j, tsl], banks[nb_][:], rb, ALU.mult),
                         reads=[bt[nb_], t_rb], writes=[t_oT])

            for f_ in proj_ops(0):
                f_()
            for j in range(16):
                fill = proj_ops(j + 1) if j + 1 < 16 else []
                scores(j, fill)
                while fill:
                    fill.pop(0)()
                pv(j)
            for a_ in range(2):
                S.op("dve", lambda a_=a_: nc.vector.memset(pTp[a_][:, 0, 0:2], 0.0), reads=ft[a_], writes=[t_pTp[a_], T_ch[11]] + ft[a_])
            S.op("dve", lambda: nc.vector.memset(d2[:, 0:1], 0.0), reads=t_rec, writes=[T_ch[14]] + t_rec)
            for i_ in range(2):
                S.op("dve", lambda i_=i_: nc.vector.memset(szp2[i_][:, 0:1], 0.0), reads=t_szp2[i_], writes=[szp_ch[i_]] + t_szp2[i_])

        phase_A(0, 0)
        for hf in range(2):
            layer0_mixer(hf)
            if nlayers == 2:
                phase_C(0, hf, hgwo_d, fillA=phase_A_ops(1, hf))
                layer1_mixer(hf)
                phase_C(1, hf, swwo_d, fillA=(phase_A_ops(0, 1) if hf == 0 else None))
            else:
                phase_C(0, hf, hgwo_d)
                if hf == 0:
                    phase_A(0, 1)

        for t in out_tiles:
            S.wait_tile_dma("sp", t)
        if dbg:
            for t in t_X1:
                S.wait_tile_dma("sp", t)
        build_program.stats = dict(cnt=dict(S.cnt), nwaits=S.nwaits, nsem=S.nsem)
    return nc


_CACHE = {}


def _consts():
    ident = np.eye(128, dtype=np.float32)
    s = np.arange(128)[:, None]
    t = np.arange(128)[None, :]
    tri = (s <= t).astype(np.float32)
    smask = np.ones((128, 512), np.float32)
    smask[:, 0::128] = 0.0
    tr = np.arange(256)[None, :]
    d = (tr - s).astype(np.float32)
    dist = np.where((d >= 0) & (d < 128), d, BIG).astype(np.float32)
    return ident, tri, smask, dist


def make_in_maps(x, c, ada_w, ada_b, ln_g, ln_b, hg_w_in, hg_w_out, hg_lb_logits, hg_norm_w,
                 sw_w_in, sw_w_out, sw_sinks):
    f = lambda a: np.ascontiguousarray(np.asarray(a, dtype=np.float32))
    ident, tri, smask, dist = _consts()
    lbl = f(np.asarray(hg_lb_logits).T.reshape(16, 128, 2).transpose(1, 0, 2))
    nw = f(np.asarray(hg_norm_w)[0].reshape(128, 1))
    sinkp = f(np.repeat(np.asarray(sw_sinks)[0].reshape(16, 2), 64, axis=1).T)
    shared = {
        "ada_w": f(ada_w), "ada_b": f(ada_b), "ln_g": f(ln_g), "ln_b": f(ln_b),
        "hg_w_in": f(np.asarray(hg_w_in)[0]), "hg_w_out": f(np.asarray(hg_w_out)[0]), "lbl": lbl, "nw": nw,
        "sw_w_in": f(np.asarray(sw_w_in)[0]), "sw_w_out": f(np.asarray(sw_w_out)[0]), "sinkp": sinkp,
        "ident": ident, "tri": tri, "smask": smask, "dist": dist,
    }
    maps = []
    for b in range(8):
        m = dict(shared)
        m["x"] = f(np.asarray(x)[b])
        m["cT"] = f(np.asarray(c)[b].reshape(8, 128).T)
        maps.append(m)
    return maps


def kernel(x, c, ada_w, ada_b, ln_g, ln_b, hg_w_in, hg_w_out, hg_lb_logits, hg_norm_w,
           sw_w_in, sw_w_out, sw_sinks):
    if "nc" not in _CACHE:
        _CACHE["nc"] = build_program()
    nc = _CACHE["nc"]
    in_maps = make_in_maps(x, c, ada_w, ada_b, ln_g, ln_b, hg_w_in, hg_w_out, hg_lb_logits, hg_norm_w,
                           sw_w_in, sw_w_out, sw_sinks)
    res = run_bass_kernel_spmd(nc, in_maps, core_ids=list(range(8)))
    out = np.stack([np.asarray(r["out"], dtype=np.float32) for r in res.results], axis=0)
    return out
```

```python
import numpy as np
from contextlib import ExitStack
import concourse.bass as bass
import concourse.mybir as mybir
from concourse.bass_utils import run_bass_kernel_spmd

F32 = mybir.dt.float32
BF16 = mybir.dt.bfloat16
AF = mybir.ActivationFunctionType
ALU = mybir.AluOpType

T = 2048
D = 1024
HALF = 1024
ALPHA = (2.0 * 2) ** 0.25
SLOPES = [float(2.0 ** (-8.0 * (i + 1) / 32)) for i in range(32)]
BIG = 1.0e6


class Tile:
    __slots__ = ("name", "w", "r", "dsem", "dcnt")

    def __init__(self, name):
        self.name = name
        self.w = None
        self.r = {}
        self.dsem = None
        self.dcnt = 0


class Sync:
    def __init__(self, nc, stack):
        self.nc = nc
        self.stack = stack
        self.eng = {"pe": nc.tensor, "act": nc.scalar, "dve": nc.vector, "pool": nc.gpsimd, "sp": nc.sync}
        self.sem = {}
        self.cnt = {}
        self.waited = {k: {} for k in self.eng}
        for k in self.eng:
            self.sem[k] = stack.enter_context(nc.semaphore("s_" + k))
            self.cnt[k] = 0
        self.nsem = 0
        self.nwaits = 0

    def new_sem(self, name):
        self.nsem += 1
        s = self.stack.enter_context(self.nc.semaphore(f"{name}_{self.nsem}"))
        self.sem[s] = s
        return s

    def _wait(self, e, key, val):
        if val <= self.waited[e].get(key, 0):
            return
        self.waited[e][key] = val
        self.eng[e].wait_ge(self.sem[key], val)
        self.nwaits += 1

    def _deps(self, e, reads, writes):
        deps = {}

        def add(d):
            if d is None:
                return
            k, v = d
            if v > deps.get(k, 0):
                deps[k] = v
        for t in reads:
            add(t.w)
        for t in writes:
            add(t.w)
            for k, v in t.r.items():
                add((k, v))
        for k, v in deps.items():
            if k == e and e == "pe":
                continue
            self._wait(e, k, v)

    def op(self, e, fn, reads=(), writes=()):
        self._deps(e, reads, writes)
        ins = fn()
        self.cnt[e] += 1
        v = self.cnt[e]
        ins.then_inc(self.sem[e], 1)
        for t in writes:
            t.w = (e, v)
            t.r = {}
        for t in reads:
            t.r[e] = v
        return ins

    def dma(self, q, out, in_, reads=(), writes=()):
        self._deps(q, reads, writes)
        owner = (list(writes) + list(reads))[0]
        if owner.dsem is None:
            owner.dsem = self.new_sem("d")
        owner.dcnt += 16
        ins = self.eng[q].dma_start(out=out, in_=in_)
        ins.then_inc(owner.dsem, 16)
        for t in writes:
            t.w = (owner.dsem, owner.dcnt)
            t.r = {}
        for t in reads:
            t.r[owner.dsem] = owner.dcnt
        return ins

    def wait_tile_dma(self, e, t):
        if t.dsem is not None:
            self._wait(e, t.dsem, t.dcnt)


def build_program(dbg=False, nlayers=2):
    nc = bass.Bass("TRN2", target_bir_lowering=False)

    def din(name, shape):
        return nc.dram_tensor(name, list(shape), F32, kind="ExternalInput").ap()

    x_d = din("x", [T, D])
    cT_d = din("cT", [128, 8])
    adaw_d = din("ada_w", [2, D, 3 * D])
    adab_d = din("ada_b", [2, 3 * D])
    lng_d = din("ln_g", [2, D])
    lnb_d = din("ln_b", [2, D])
    hgwi_d = din("hg_w_in", [D, 8192])
    hgwo_d = din("hg_w_out", [2048, D])
    lbl_d = din("lbl", [128, 16, 2])
    nw_d = din("nw", [128, 1])
    swwi_d = din("sw_w_in", [D, 4608])
    swwo_d = din("sw_w_out", [2048, D])
    sink_d = din("sinkp", [128, 16])
    ident_d = din("ident", [128, 128])
    tri_d = din("tri", [128, 128])
    smask_d = din("smask", [128, 512])
    dist_d = din("dist", [128, 256])
    out_d = nc.dram_tensor("out", [T, D], F32, kind="ExternalOutput").ap()
    dbg_d = nc.dram_tensor("dbg", [T, D], F32, kind="ExternalOutput").ap() if dbg else None

    with ExitStack() as st:
        S = Sync(nc, st)

        def sb(name, shape, dt=F32):
            return st.enter_context(nc.sbuf_tensor("sb_" + name, list(shape), dt))

        banks = [st.enter_context(nc.psum_tensor(f"bank{i}", [128, 512], F32)) for i in range(8)]
        bt = [Tile(f"bank{i}") for i in range(8)]

        ident_f = sb("ident_f", [128, 128]); t_identf = Tile("ident_f")
        ident_b = sb("ident_b", [128, 128], BF16); t_identb = Tile("ident_b")
        tri_b = sb("tri_b", [128, 128], BF16); t_tri = Tile("tri")
        smask = sb("smask", [128, 512]); t_smask = Tile("smask")
        distm = sb("distm", [128, 256]); t_dist = Tile("dist")
        onesm = sb("onesm", [128, 128], BF16); t_onesm = Tile("onesm")
        oz = sb("oz", [128, 192], BF16); t_oz = Tile("oz")
        cT = sb("cT", [128, 8]); t_cT = Tile("cT")
        cact = sb("cact", [128, 8]); t_cact = Tile("cact")
        cbc = sb("cbc", [128, 8, 128], BF16); t_cbc = Tile("cbc")
        lbl = sb("lbl", [128, 16, 2]); t_lbl = Tile("lbl")
        ldiff = sb("ldiff", [128, 16]); t_ldiff = Tile("ldiff")
        oml = sb("oml", [128, 16]); t_oml = Tile("oml")
        lbm1 = sb("lbm1", [128, 16]); t_lbm1 = Tile("lbm1")
        nw = sb("nw", [128, 1]); t_nw = Tile("nw")
        sinkp = sb("sinkp", [128, 16]); t_sinkp = Tile("sinkp")
        esink = sb("esink", [128, 16]); t_esink = Tile("esink")
        g1p = [sb(f"g1p{L}", [128, D]) for L in range(2)]; t_g1p = [Tile(f"g1p{L}") for L in range(2)]
        modfm = [sb(f"modfm{L}", [128, 2, 8]) for L in range(2)]; t_modfm = [Tile(f"modfm{L}") for L in range(2)]
        S_all = sb("S_all", [128, 16, 128]); t_S = [Tile(f"S{h}") for h in range(16)]
        NSLOT = 6
        slots = [sb(f"slot{i}", [128, 4096], BF16) for i in range(NSLOT)]
        t_slot = [Tile(f"slot{i}") for i in range(NSLOT)]
        hT = sb("hT", [128, 8, HALF], BF16); t_hT = [Tile("hT0"), Tile("hT1")]
        oT = sb("oT", [128, 16, HALF], BF16); t_oT = Tile("oT")
        stC = sb("stC", [128, 2, 6]); t_stC = Tile("stC")
        mvC = sb("mvC", [128, 2]); t_mvC = Tile("mvC")
        smC = sb("smC", [128, 4]); t_smC = Tile("smC")
        Stb = [sb(f"Stb{i}", [128, 128], BF16) for i in range(2)]; t_Stb = [Tile(f"Stb{i}") for i in range(2)]
        ar4 = [sb(f"ar4{i}", [128, 2, 4]) for i in range(2)]; t_ar4 = [Tile(f"ar4{i}") for i in range(2)]
        kT = sb("kT", [128, 4, 128 + HALF], BF16); t_kT = Tile("kT")
        vz = sb("vz", [128, 9, 4, 192], BF16); t_vz = Tile("vz")

        NCH = 15
        arena = sb("arena", [128, NCH * 1024])
        arena_b = sb_alias = None
        T_ch = [Tile(f"ch{i}") for i in range(NCH)]

        def f32v(ch, off, n):
            return arena[:, ch * 1024 + off: ch * 1024 + off + n]

        def bf16v(ch, off, n, nch=1):
            return arena[:, ch * 1024:(ch + nch) * 1024].bitcast(BF16)[:, off:off + n]

        xin = [f32v(11 + i, 0, 1024) for i in range(4)]; t_xin = [T_ch[11 + i] for i in range(4)]
        NXIN = 4
        lnp_g = f32v(8, 0, 1024); lnp_b = f32v(9, 0, 1024)
        tC = f32v(10, 0, 1024); t_tC = T_ch[10]
        X1 = [f32v(i, 0, 1024) for i in range(8)]; t_X1 = [T_ch[i] for i in range(8)]
        v_tm = bf16v(0, 0, 4096, nch=2).rearrange("p (t n) -> p t n", n=512)
        t_vtm = [T_ch[0]] * 4 + [T_ch[1]] * 4
        kp = [f32v(2, 0, 1024), f32v(3, 0, 1024)]; t_kp = [T_ch[2], T_ch[3]]
        szb = [bf16v(4, 0, 1024), bf16v(14, 0, 1024)]; t_szb = [T_ch[4], T_ch[14]]
        sgz = f32v(5, 0, 512); t_sgz = T_ch[5]
        gg = [f32v(6 + 4 * s_, 0, 512) for s_ in range(2)]; t_gg = [T_ch[6 + 4 * s_] for s_ in range(2)]
        bb = [f32v(6 + 4 * s_, 512, 512) for s_ in range(2)]; t_bb = t_gg
        E1 = [f32v(7 + 4 * s_, 0, 512) for s_ in range(2)]; t_E1 = [T_ch[7 + 4 * s_] for s_ in range(2)]
        E2 = [f32v(7 + 4 * s_, 512, 512) for s_ in range(2)]; t_E2 = t_E1
        rs = [f32v(8 + 4 * s_, 0, 512) for s_ in range(2)]; t_rs = [T_ch[8 + 4 * s_] for s_ in range(2)]
        pT = [bf16v(8 + 4 * s_, 1024, 512) for s_ in range(2)]; t_pT = t_rs
        sq = [bf16v(8 + 4 * s_, 1536, 512) for s_ in range(2)]; t_sq = t_rs
        qt = [bf16v(9 + 4 * s_, 0, 512) for s_ in range(2)]; t_qt = [T_ch[9 + 4 * s_] for s_ in range(2)]
        kt = [bf16v(9 + 4 * s_, 512, 512) for s_ in range(2)]; t_kt = t_qt
        kh = [bf16v(9 + 4 * s_, 1024, 512) for s_ in range(2)]; t_kh = t_qt
        khT = [bf16v(9 + 4 * s_, 1536, 512) for s_ in range(2)]; t_khT = t_qt
        wk_rep = bf16v(8, 0, 4096, nch=2).rearrange("p (c g n) -> p c g n", g=4, n=128); t_wkrep = [T_ch[8], T_ch[9]]
        szp = f32v(10, 0, 1024); t_szp = T_ch[10]
        qTp = bf16v(11, 0, 1024); t_qTp = T_ch[11]
        Mh = [bf16v(11, 1024 + a_ * 256, 256) for a_ in range(2)]; t_Mh = [T_ch[11], T_ch[11]]
        pTh = [bf16v(11, 1536 + a_ * 128, 128) for a_ in range(2)]
        pTp = [bf16v(12 + a_, 0, 2048).rearrange("p (k t) -> p k t", t=256) for a_ in range(2)]
        t_pTp = [T_ch[12], T_ch[13]]
        d2 = f32v(14, 0, 512); wdiv = f32v(14, 512, 512); t_d2 = T_ch[14]; t_wdiv = T_ch[14]

        wstate = {"n": 0}

        def wload(src_ap, view):
            i = wstate["n"] % NSLOT
            wstate["n"] += 1
            if view == "in":
                v = slots[i][:, :].rearrange("p (c n) -> p c n", n=512)
            else:
                v = slots[i][:, :].rearrange("p (c n) -> p c n", n=1024)
            S.dma("pool", v, src_ap, writes=[t_slot[i]])
            return v, t_slot[i]

        def w_in_block(w_d, col0):
            return w_d[:, col0:col0 + 512].rearrange("(c p) n -> p c n", p=128)

        def w_out_block(w_d, k):
            return w_d[k * 512:(k + 1) * 512, :].rearrange("(e p) n -> p e n", p=128)

        S.dma("sp", ident_f[:], ident_d, writes=[t_identf])
        S.dma("sp", cT[:], cT_d, writes=[t_cT])
        S.dma("sp", lbl[:], lbl_d, writes=[t_lbl])
        S.dma("sp", nw[:], nw_d, writes=[t_nw])
        S.dma("sp", sinkp[:], sink_d, writes=[t_sinkp])
        S.dma("sp", smask[:], smask_d, writes=[t_smask])
        S.dma("sp", distm[:], dist_d, writes=[t_dist])
        S.dma("pool", ident_b[:], ident_d, writes=[t_identb])
        S.dma("pool", tri_b[:], tri_d, writes=[t_tri])
        S.op("dve", lambda: nc.vector.memset(onesm[:], 1.0 / 128.0), writes=[t_onesm])
        S.op("dve", lambda: nc.vector.memset(oz[:], 0.0), writes=[t_oz])
        S.op("dve", lambda: nc.vector.memset(oz[:, 64:128], 1.0), reads=[t_oz], writes=[t_oz])
        S.op("dve", lambda: nc.vector.memset(S_all[:], 0.0), writes=t_S)
        S.op("dve", lambda: nc.vector.memset(vz[:], 0.0), writes=[t_vz])
        S.op("dve", lambda: nc.vector.memset(kT[:], 0.0), writes=[t_kT])
        S.op("act", lambda: nc.scalar.activation(cact[:], cT[:], AF.Silu), reads=[t_cT], writes=[t_cact])
        S.op("dve", lambda: nc.vector.tensor_copy(cbc[:], cact[:, :].unsqueeze(2).to_broadcast([128, 8, 128])),
             reads=[t_cact], writes=[t_cbc])
        S.op("dve", lambda: nc.vector.tensor_tensor(ldiff[:, :].unsqueeze(2), lbl[:, :, 0:1], lbl[:, :, 1:2], ALU.subtract),
             reads=[t_lbl], writes=[t_ldiff])
        S.op("act", lambda: nc.scalar.activation(oml[:], ldiff[:], AF.Sigmoid, scale=-1.0), reads=[t_ldiff], writes=[t_oml])
        S.op("dve", lambda: nc.vector.tensor_scalar(lbm1[:], oml[:], -1.0, None, ALU.mult), reads=[t_oml], writes=[t_lbm1])
        S.op("act", lambda: nc.scalar.activation(esink[:], sinkp[:], AF.Exp), reads=[t_sinkp], writes=[t_esink])

        modb = tC
        for L in range(nlayers):
            S.dma("sp", g1p[L][:], adab_d[L:L + 1, 2 * D:3 * D].partition_broadcast(128), writes=[t_g1p[L]])
            for part in range(3):
                if part < 2:
                    S.dma("sp", modb, adab_d[L:L + 1, part * D:(part + 1) * D].partition_broadcast(128), writes=[t_tC])
                    dst, t_dst = modb, t_tC
                else:
                    dst, t_dst = g1p[L], t_g1p[L]
                addc = 0.0 if part == 0 else 1.0
                for nb in range(2):
                    wv, t_w = wload(w_in_block(adaw_d[L], part * D + nb * 512), "in")
                    bk = nb
                    for dc in range(8):
                        S.op("pe", lambda dc=dc, wv=wv, bk=bk: nc.tensor.matmul(
                            banks[bk][:], cbc[:, dc, :], wv[:, dc, :], start=(dc == 0), stop=(dc == 7)),
                            reads=[t_cbc, t_w], writes=[bt[bk]])
                    S.op("dve", lambda nb=nb, bk=bk, dst=dst, addc=addc: nc.vector.scalar_tensor_tensor(
                        dst[:, nb * 512:(nb + 1) * 512], banks[bk][:], addc, dst[:, nb * 512:(nb + 1) * 512],
                        ALU.add, ALU.add), reads=[bt[bk], t_dst], writes=[t_dst])
                if part < 2:
                    trv = banks[2][:, 0:256].rearrange("p (c n) -> p c n", n=32)
                    for dc in range(8):
                        S.op("pe", lambda dc=dc, trv=trv: nc.tensor.transpose(
                            trv[:, dc, :], modb[0:32, dc * 128:(dc + 1) * 128], ident_f[0:32, 0:32]),
                            reads=[t_tC, t_identf], writes=[bt[2]])
                    S.op("dve", lambda L=L, part=part, trv=trv: nc.vector.tensor_copy(
                        modfm[L][:, part, :].unsqueeze(2), trv[:, :, 0:1]), reads=[bt[2]], writes=[t_modfm[L]])

        xin_state = {"n": 0}

        def load_x_tile(gt):
            i = xin_state["n"] % NXIN
            xin_state["n"] += 1
            S.dma("sp", xin[i], x_d[gt * 128:(gt + 1) * 128, :], writes=[t_xin[i]])
            return xin[i], t_xin[i]

        def phase_A_ops(L, hf):
            out = [[], []]
            for tb in range(2):
                srcs = []

                def get_srcs(tb=tb, srcs=srcs):
                    if not srcs:
                        for i in range(4):
                            tt = tb * 4 + i
                            if L == 0:
                                srcs.append(load_x_tile(hf * 8 + tt))
                            else:
                                srcs.append((X1[tt], t_X1[tt]))
                    return srcs
                for dc in range(8):
                    def op_(dc=dc, tb=tb, get_srcs=get_srcs):
                        sr = get_srcs()
                        bk = 5 + dc % 2
                        for i in range(4):
                            xa, xt = sr[i]
                            S.op("pe", lambda xa=xa, i=i: nc.tensor.transpose(
                                banks[bk][:, i * 128:(i + 1) * 128], xa[:, dc * 128:(dc + 1) * 128], ident_f[:]),
                                reads=[xt, t_identf], writes=[bt[bk]])
                        S.op("act", lambda: nc.scalar.activation(
                            hT[:, dc, tb * 512:(tb + 1) * 512], banks[bk][:], AF.Identity,
                            bias=modfm[L][:, 0, dc:dc + 1], scale=modfm[L][:, 1, dc:dc + 1]),
                            reads=[bt[bk], t_modfm[L]], writes=[t_hT[tb]])
                    out[tb].append(op_)
            return out

        def phase_A(L, hf):
            for lst in phase_A_ops(L, hf):
                for f_ in lst:
                    f_()

        def phase_C(L, hf, w_d, fillA=None):
            S.dma("sp", lnp_g, lng_d[L:L + 1, :].partition_broadcast(128), writes=[T_ch[8]])
            S.dma("sp", lnp_b, lnb_d[L:L + 1, :].partition_broadcast(128), writes=[T_ch[9]])
            wo = [WO_PRE.pop((L, hf, k)) if (L, hf, k) in WO_PRE else wload(w_out_block(w_d, k), "out") for k in range(4)]
            last = (L == nlayers - 1)
            pend_fin = []
            pend_mid = []
            for tt in range(8):
                gt = hf * 8 + tt
                if L == 0:
                    xa, xt = load_x_tile(gt)
                else:
                    xa, xt = X1[tt], t_X1[tt]
                for nb in range(2):
                    bk = (tt % 2) * 2 + nb
                    for e in range(16):
                        wv, t_w = wo[e // 4]
                        S.op("pe", lambda e=e, wv=wv, bk=bk, nb=nb, tt=tt: nc.tensor.matmul(
                            banks[bk][:], oT[:, e, tt * 128:(tt + 1) * 128], wv[:, e % 4, nb * 512:(nb + 1) * 512],
                            start=(e == 0), stop=(e == 15)), reads=[t_oT, t_w], writes=[bt[bk]])
                    S.op("dve", lambda bk=bk, nb=nb: nc.vector.tensor_tensor(
                        tC[:, nb * 512:(nb + 1) * 512], banks[bk][:], g1p[L][:, nb * 512:(nb + 1) * 512], ALU.mult),
                        reads=[bt[bk], t_g1p[L]], writes=[t_tC])
                S.op("dve", lambda xa=xa: nc.vector.scalar_tensor_tensor(xa, xa, ALPHA, tC, ALU.mult, ALU.add),
                     reads=[xt, t_tC], writes=[xt])
                while pend_mid:
                    pend_mid.pop(0)()
                for nb in range(2):
                    S.op("dve", lambda nb=nb, xa=xa: nc.vector.bn_stats(stC[:, nb, :], xa[:, nb * 512:(nb + 1) * 512]),
                         reads=[xt], writes=[t_stC])
                S.op("dve", lambda: nc.vector.bn_aggr(mvC[:], stC[:, :, :].rearrange("p a b -> p (a b)")),
                     reads=[t_stC], writes=[t_mvC])
                while pend_fin:
                    pend_fin.pop(0)()
                S.op("act", lambda: nc.scalar.activation(smC[:, 0:1], mvC[:, 1:2], AF.Ln, bias=1e-5, scale=1.0),
                     reads=[t_mvC], writes=[t_smC])
                S.op("act", lambda: nc.scalar.activation(smC[:, 1:2], smC[:, 0:1], AF.Exp, scale=-0.5),
                     reads=[t_smC], writes=[t_smC])
                def mid(xa=xa, xt=xt):
                    S.op("dve", lambda: nc.vector.tensor_scalar(smC[:, 2:3], mvC[:, 0:1], smC[:, 1:2], -1.0, ALU.mult, ALU.mult),
                         reads=[t_mvC, t_smC], writes=[t_smC])
                    S.op("act", lambda: nc.scalar.activation(xa, xa, AF.Identity, bias=smC[:, 2:3], scale=smC[:, 1:2]),
                         reads=[xt, t_smC], writes=[xt])
                pend_mid.append(mid)
                def fin(xa=xa, xt=xt, tt=tt, gt=gt):
                    S.op("dve", lambda: nc.vector.tensor_tensor(xa, xa, lnp_g, ALU.mult),
                         reads=[xt, T_ch[8]], writes=[xt])
                    if not last:
                        S.op("dve", lambda: nc.vector.tensor_tensor(X1[tt], xa, lnp_b, ALU.add),
                             reads=[xt, T_ch[9]], writes=[t_X1[tt]])
                        if dbg:
                            S.dma("sp", dbg_d[gt * 128:(gt + 1) * 128, :], X1[tt], reads=[t_X1[tt]])
                    else:
                        S.op("dve", lambda: nc.vector.tensor_tensor(xa, xa, lnp_b, ALU.add),
                             reads=[xt, T_ch[9]], writes=[xt])
                        S.dma("sp", out_d[gt * 128:(gt + 1) * 128, :], xa, reads=[xt])
                        out_tiles.add(xt)
                pend_fin.append(fin)
                if tt == 7:
                    while pend_mid:
                        pend_mid.pop(0)()
                    while pend_fin:
                        pend_fin.pop(0)()
                if fillA is not None:
                    if L == 0:
                        if tt >= 4:
                            for _ in range(2):
                                if fillA[0]:
                                    fillA[0].pop(0)()
                        if tt == 7:
                            for lst in fillA:
                                while lst:
                                    lst.pop(0)()
                    else:
                        lst = fillA[0] if fillA[0] else fillA[1]
                        for _ in range(2):
                            if lst:
                                lst.pop(0)()
            while pend_mid:
                pend_mid.pop(0)()
            while pend_fin:
                pend_fin.pop(0)()

        out_tiles = set()
        WO_PRE = {}

        def layer0_mixer(hf):
            from collections import deque
            FQ = deque()
            st_ = {"enq": 0, "pop": 0}
            marks = {}

            def enq(fn):
                FQ.append(fn)
                st_["enq"] += 1

            def run(n, limit):
                while n > 0 and FQ and st_["pop"] < limit:
                    FQ.popleft()()
                    st_["pop"] += 1
                    n -= 1

            def flush_to(mark):
                while st_["pop"] < mark:
                    FQ.popleft()()
                    st_["pop"] += 1
                    tick()

            DQ = []

            def defer(fn, k):
                DQ.append([k, fn])

            def tick():
                try_loads()
                due = [d for d in DQ if d[0] <= 1]
                rest = [d for d in DQ if d[0] > 1]
                DQ[:] = rest
                for d in rest:
                    d[0] -= 1
                for d in due:
                    d[1]()

            def drain_deferred():
                while DQ:
                    tick()

            t_vtm = [Tile(f"vtm{i_}") for i_ in range(8)]
            t_kp = [[Tile(f"kp{i_}{j_}") for j_ in range(2)] for i_ in range(2)]
            t_szb = [[Tile(f"szb{i_}{j_}") for j_ in range(2)] for i_ in range(2)]
            t_sgz = Tile("sgz")
            mk2 = lambda n_: [Tile(f"{n_}{i_}") for i_ in range(2)]
            t_gg, t_bb, t_E1, t_E2 = mk2("gg"), mk2("bb"), mk2("E1"), mk2("E2")
            t_rs, t_pT, t_sq = mk2("rs"), mk2("pT"), mk2("sq")
            t_qt, t_kt, t_kh, t_khT = mk2("qt"), mk2("kt"), mk2("kh"), mk2("khT")
            fine = {0: t_vtm[0:4], 1: t_vtm[4:8], 2: t_kp[0], 3: t_kp[1], 4: t_szb[0], 5: [t_sgz], 14: t_szb[1]}
            for s_ in range(2):
                fine[6 + 4 * s_] = [t_gg[s_], t_bb[s_]]
                fine[7 + 4 * s_] = [t_E1[s_], t_E2[s_]]
                fine[8 + 4 * s_] = [t_rs[s_], t_pT[s_], t_sq[s_]]
                fine[9 + 4 * s_] = [t_qt[s_], t_kt[s_], t_kh[s_], t_khT[s_]]
            for ch_, tl_ in fine.items():
                S.op("dve", lambda ch_=ch_: nc.vector.memset(f32v(ch_, 0, 1), 0.0), writes=[T_ch[ch_]] + tl_)
            for bi_ in range(2):
                S.op("dve", lambda bi_=bi_: nc.vector.memset(pT[bi_], 0.0), writes=[t_pT[bi_]])

            W = {}
            WOFF = {"f": 2048, "z": 6144, "q": 0, "v": 4096}
            load_order = [("v", 0), ("f", 0), ("z", 0), ("q", 0)]
            for g_ in range(1, 4):
                load_order += [("f", g_), ("z", g_), ("q", g_), ("v", g_)]
            load_order += [("wo", k_) for k_ in range(4)]
            uses = {k_: 0 for k_ in load_order}
            lstate = {"next": 0}

            def try_loads():
                while lstate["next"] < len(load_order):
                    i = lstate["next"]
                    key = load_order[i]
                    if i >= NSLOT and uses[load_order[i - NSLOT]] < 64:
                        break
                    if key[0] == "wo":
                        WO_PRE[(0, hf, key[1])] = wload(w_out_block(hgwo_d, key[1]), "out")
                    else:
                        W[key] = wload(w_in_block(hgwi_d, WOFF[key[0]] + key[1] * 512), "in")
                    lstate["next"] += 1

            def load_w(kind, grp):
                assert (kind, grp) in W, (kind, grp)
                uses[(kind, grp)] += 1
                return W[(kind, grp)]

            def V(grp):
                try_loads()
                wvv, t_wv = load_w("v", grp)
                uses[("v", grp)] += 63
                for tt in range(8):
                    bk = tt % 2
                    for dc in range(8):
                        S.op("pe", lambda dc=dc, bk=bk, tt=tt: nc.tensor.matmul(
                            banks[bk][:], hT[:, dc, tt * 128:(tt + 1) * 128], wvv[:, dc, :],
                            start=(dc == 0), stop=(dc == 7)), reads=[t_hT[tt // 4], t_wv], writes=[bt[bk]])
                    if tt % 2 == 0:
                        S.op("act", lambda bk=bk, tt=tt: nc.scalar.copy(v_tm[:, tt, :], banks[bk][:]),
                             reads=[bt[bk]], writes=[t_vtm[tt]])
                    else:
                        S.op("dve", lambda bk=bk, tt=tt: nc.vector.tensor_copy(v_tm[:, tt, :], banks[bk][:]),
                             reads=[bt[bk]], writes=[t_vtm[tt]])

            def enq_A(head, blk):
                grp, hh = divmod(head, 4)
                cs0 = hh * 128
                hp = head % 2
                if True:
                    tsl = slice(blk * 512, (blk + 1) * 512)
                    for dc in range(8):
                        def f_(dc=dc, tsl=tsl, blk=blk):
                            wf, t_wf = load_w("f", grp)
                            S.op("pe", lambda: nc.tensor.matmul(
                                banks[0][:], wf[:, dc, cs0:cs0 + 128], hT[:, dc, tsl], start=(dc == 0), stop=(dc == 7)),
                                reads=[t_hT[blk], t_wf], writes=[bt[0]])
                            if dc == 7:
                                S.op("act", lambda: nc.scalar.activation(kp[hp][:, tsl], banks[0][:], AF.Exp),
                                     reads=[bt[0]], writes=[t_kp[hp][blk]])
                                S.op("act", lambda: nc.scalar.activation(kp[hp][:, tsl], kp[hp][:, tsl], AF.Ln, bias=1.0, scale=1.0),
                                     reads=[t_kp[hp][blk]], writes=[t_kp[hp][blk]])
                                S.op("act", lambda: nc.scalar.activation(kp[hp][:, tsl], kp[hp][:, tsl], AF.Exp, scale=-1.0),
                                     reads=[t_kp[hp][blk]], writes=[t_kp[hp][blk]])
                        enq(f_)
                    for dc in range(8):
                        def z_(dc=dc, tsl=tsl, blk=blk):
                            wz, t_wz = load_w("z", grp)
                            S.op("pe", lambda: nc.tensor.matmul(
                                banks[1][:], wz[:, dc, cs0:cs0 + 128], hT[:, dc, tsl], start=(dc == 0), stop=(dc == 7)),
                                reads=[t_hT[blk], t_wz], writes=[bt[1]])
                            if dc == 7:
                                S.op("act", lambda: nc.scalar.activation(sgz, banks[1][:], AF.Exp, scale=-1.0),
                                     reads=[bt[1]], writes=[t_sgz])
                                S.op("act", lambda: nc.scalar.activation(sgz, sgz, AF.Ln, bias=1.0, scale=1.0),
                                     reads=[t_sgz], writes=[t_sgz])
                                S.op("act", lambda: nc.scalar.activation(sgz, sgz, AF.Exp, scale=-1.0),
                                     reads=[t_sgz], writes=[t_sgz])
                                defer(lambda: S.op("dve", lambda: nc.vector.tensor_tensor(
                                    szb[hp][:, tsl], banks[1][:], sgz, ALU.mult),
                                    reads=[bt[1], t_sgz], writes=[t_szb[hp][blk]]), 3)
                        enq(z_)
                marks[("A", head, blk)] = st_["enq"]

            def gating(head, blk, piece):
                hp = head % 2
                bi = blk
                qb = 2
                tsl = slice(blk * 512, (blk + 1) * 512)
                b3 = bb[bi].rearrange("p (c t) -> p c t", t=128)
                g3 = gg[bi].rearrange("p (c t) -> p c t", t=128)
                if piece == 0:
                    S.op("act", lambda: nc.scalar.activation(
                        gg[bi], kp[hp][:, tsl], AF.Ln, bias=1.0, scale=lbm1[:, head:head + 1]),
                        reads=[t_kp[hp][blk], t_lbm1], writes=[t_gg[bi]])
                elif piece == 1:
                    S.op("dve", lambda: nc.vector.tensor_tensor_scan(bb[bi], smask[:], gg[bi], 0.0, ALU.mult, ALU.add),
                         reads=[t_smask, t_gg[bi]], writes=[t_bb[bi]])
                elif piece == 2:
                    S.op("act", lambda: nc.scalar.activation(ar4[bi][:, 0, :].unsqueeze(2), b3[:, :, 127:128], AF.Exp),
                         reads=[t_bb[bi]], writes=[t_ar4[bi]])
                    S.op("act", lambda: nc.scalar.activation(ar4[bi][:, 1, :].unsqueeze(2), b3[:, :, 63:64], AF.Exp),
                         reads=[t_bb[bi]], writes=[t_ar4[bi]])
                    S.op("pool", lambda: nc.gpsimd.tensor_tensor(
                        g3, b3, b3[:, :, 63:64].to_broadcast([128, 4, 128]), ALU.subtract),
                        reads=[t_bb[bi]], writes=[t_gg[bi]])
                elif piece == 3:
                    S.op("act", lambda: nc.scalar.activation(E1[bi], gg[bi], AF.Exp), reads=[t_gg[bi]], writes=[t_E1[bi]])
                    S.op("act", lambda: nc.scalar.activation(E2[bi], gg[bi], AF.Exp, scale=-1.0),
                         reads=[t_gg[bi]], writes=[t_E2[bi]])
                elif piece == 4:
                    S.op("dve", lambda: nc.vector.scalar_tensor_tensor(
                        qt[bi], banks[qb][:], oml[:, head:head + 1], E1[bi], ALU.mult, ALU.mult),
                        reads=[bt[qb], t_oml, t_E1[bi]], writes=[t_qt[bi]])
                else:
                    S.op("pool", lambda: nc.gpsimd.tensor_tensor(kt[bi], kp[hp][:, tsl], E2[bi], ALU.mult),
                         reads=[t_kp[hp][blk], t_E2[bi]], writes=[t_kt[bi]])
                    e13 = E1[bi].rearrange("p (c t) -> p c t", t=128)
                    S.op("pool", lambda: nc.gpsimd.tensor_tensor(
                        kh[bi].rearrange("p (c t) -> p c t", t=128), kt[bi].rearrange("p (c t) -> p c t", t=128),
                        e13[:, :, 127:128].to_broadcast([128, 4, 128]), ALU.mult),
                        reads=[t_kt[bi], t_E1[bi]], writes=[t_kh[bi]])

            def enq_B(head, blk):
                grp, hh = divmod(head, 4)
                cs0 = hh * 128
                qb = 2
                tsl = slice(blk * 512, (blk + 1) * 512)
                marks[("Bs", head, blk)] = st_["enq"]
                for dc in range(8):
                    def q_(dc=dc):
                        wq, t_wq = load_w("q", grp)
                        S.op("pe", lambda: nc.tensor.matmul(
                            banks[qb][:], wq[:, dc, cs0:cs0 + 128], hT[:, dc, tsl], start=(dc == 0), stop=(dc == 7)),
                            reads=[t_hT[blk], t_wq], writes=[bt[qb]])
                    enq(q_)
                marks[("B", head, blk)] = st_["enq"]

            def C_head(head, blk, limit, G):
                bi = blk
                for c in range(4):
                    c0 = c * 128
                    S.op("pe", lambda c0=c0: nc.tensor.matmul(
                        banks[5][:, c0 + 64:c0 + 128], kt[bi][:, c0:c0 + 128], qt[bi][:, c0 + 64:c0 + 128],
                        start=True, stop=True), reads=[t_kt[bi], t_qt[bi]], writes=[bt[5]])
                    S.op("pe", lambda c0=c0: nc.tensor.matmul(
                        banks[5][0:64, c0:c0 + 64], kt[bi][:, c0:c0 + 64], qt[bi][:, c0:c0 + 64],
                        start=True, stop=True), reads=[t_kt[bi], t_qt[bi]], writes=[bt[5]])
                trk = banks[4][:, 0:256].bitcast(BF16)
                for c in range(4):
                    S.op("pe", lambda c=c: nc.tensor.transpose(
                        trk[:, c * 128:(c + 1) * 128], kh[bi][:, c * 128:(c + 1) * 128], ident_b[:]),
                        reads=[t_kh[bi], t_identb], writes=[bt[4]])
                p3 = pT[bi].rearrange("p (c t) -> p c t", t=128)
                s3 = banks[5][:, :].rearrange("p (c t) -> p c t", t=128)
                S.op("dve", lambda: nc.vector.tensor_tensor(
                    p3[:, :, 64:128], s3[:, :, 64:128],
                    tri_b[:, 64:128].unsqueeze(1).to_broadcast([128, 4, 64]), ALU.mult),
                    reads=[bt[5], t_tri], writes=[t_pT[bi]])
                S.op("dve", lambda: nc.vector.tensor_tensor(
                    p3[0:64, :, 0:64], s3[0:64, :, 0:64],
                    tri_b[0:64, 0:64].unsqueeze(1).to_broadcast([64, 4, 64]), ALU.mult),
                    reads=[bt[5], t_tri], writes=[t_pT[bi]])
                S.op("dve", lambda: nc.vector.tensor_copy(khT[bi], trk), reads=[bt[4]], writes=[t_khT[bi]])

            def C_steps(head, blk, limit, G):
                grp, hh = divmod(head, 4)
                cs0 = hh * 128
                bi = blk
                ob = 6 if blk == 0 else 3
                G("n2")
                tick()
                run(4, limit)
                for c in range(4):
                    first = (hf == 0 and blk == 0 and c == 0)
                    csl = slice(c * 128, (c + 1) * 128)
                    vt = blk * 4 + c
                    si = c % 2
                    if not first:
                        S.op("dve", lambda si=si, c=c: nc.vector.tensor_scalar(
                            Stb[si][:], S_all[:, head, :], ar4[bi][:, 1, c:c + 1], None, ALU.mult),
                            reads=[t_S[head], t_ar4[bi]], writes=[t_Stb[si]])
                    S.op("pe", lambda csl=csl, vt=vt, first=first: nc.tensor.matmul(
                        banks[ob][:, csl], v_tm[:, vt, cs0:cs0 + 128], pT[bi][:, csl], start=True, stop=first),
                        reads=[t_vtm[vt], t_pT[bi]], writes=[bt[ob]])
                    S.op("pe", lambda csl=csl, vt=vt: nc.tensor.matmul(
                        banks[4][:, 256:384], khT[bi][:, csl], v_tm[:, vt, cs0:cs0 + 128], start=True, stop=True),
                        reads=[t_khT[bi], t_vtm[vt]], writes=[bt[4]])
                    if not first:
                        S.op("pe", lambda csl=csl, si=si: nc.tensor.matmul(
                            banks[ob][:, csl], Stb[si][:], qt[bi][:, csl], start=False, stop=True),
                            reads=[t_Stb[si], t_qt[bi]], writes=[bt[ob]])
                    if first:
                        S.op("dve", lambda: nc.vector.tensor_copy(S_all[:, head, :], banks[4][:, 256:384]),
                             reads=[bt[4]], writes=[t_S[head]])
                    else:
                        S.op("dve", lambda c=c: nc.vector.scalar_tensor_tensor(
                            S_all[:, head, :], S_all[:, head, :], ar4[bi][:, 0, c:c + 1], banks[4][:, 256:384],
                            ALU.mult, ALU.add), reads=[t_S[head], t_ar4[bi], bt[4]], writes=[t_S[head]])
                    tick()
                    run(5, limit)
                    for p_ in (("n3",), ("n5",), ("nn0",), ("n4", "nn1"))[c]:
                        G(p_)

            def C_tail(head, blk):
                hp = head % 2
                bi = blk
                ob = 6 if blk == 0 else 3
                tsl = slice(blk * 512, (blk + 1) * 512)
                S.op("act", lambda: nc.scalar.activation(sq[bi], banks[ob][:], AF.Square), reads=[bt[ob]], writes=[t_sq[bi]])

                def st1():
                    S.op("pe", lambda: nc.tensor.matmul(banks[7][:], onesm[:], sq[bi], start=True, stop=True),
                         reads=[t_onesm, t_sq[bi]], writes=[bt[7]])

                def st2():
                    S.op("act", lambda: nc.scalar.activation(rs[bi], banks[7][:], AF.Ln, bias=1e-6, scale=1.0),
                         reads=[bt[7]], writes=[t_rs[bi]])
                    S.op("act", lambda: nc.scalar.activation(rs[bi], rs[bi], AF.Exp, scale=-0.5),
                         reads=[t_rs[bi]], writes=[t_rs[bi]])

                def st3():
                    S.op("dve", lambda: nc.vector.scalar_tensor_tensor(
                        rs[bi], banks[ob][:], nw[:, 0:1], rs[bi], ALU.mult, ALU.mult),
                        reads=[bt[ob], t_rs[bi], t_nw], writes=[t_rs[bi]])
                    S.op("pool", lambda: nc.gpsimd.tensor_tensor(oT[:, head, tsl], rs[bi], szb[hp][:, tsl], ALU.mult),
                         reads=[t_rs[bi], t_szb[hp][blk]], writes=[t_oT])
                defer(st1, 1)
                defer(st2, 2)
                defer(st3, 3)

            V(0)
            enq_A(0, 0)
            enq_B(0, 0)
            enq_A(0, 1)
            flush_to(st_["enq"])
            drain_deferred()
            for piece in range(6):
                gating(0, 0, piece)
            gating(0, 1, 0)
            gating(0, 1, 1)
            units = [(h, b_) for h in range(16) for b_ in range(2)]
            C_head(0, 0, 0, None)
            for ui, (head, blk) in enumerate(units):
                nh = head + 1
                if blk == 0:
                    if nh < 16:
                        enq_A(nh, 0)
                    enq_B(head, 1)
                else:
                    if nh < 16:
                        enq_A(nh, 1)
                        enq_B(nh, 0)
                nu = units[ui + 1] if ui + 1 < len(units) else None
                nnu = units[ui + 2] if ui + 2 < len(units) else None
                limit = st_["enq"]

                def G(tag, nu=nu, nnu=nnu):
                    u_ = nnu if tag.startswith("nn") else nu
                    if u_ is not None:
                        gating(u_[0], u_[1], int(tag[-1]))
                C_steps(head, blk, limit, G)
                flush_to(st_["enq"])
                new_group = (blk == 1 and nh % 4 == 0 and nh < 16)
                if new_group:
                    C_tail(head, blk)
                    drain_deferred()
                    V(nh // 4)
                    C_head(nh, 0, 0, None)
                else:
                    if nu is not None:
                        C_head(nu[0], nu[1], 0, None)
                    C_tail(head, blk)
            flush_to(st_["enq"])
            drain_deferred()
            for ch_, tl_ in fine.items():
                S.op("dve", lambda ch_=ch_: nc.vector.memset(f32v(ch_, 0, 1), 0.0), reads=tl_, writes=[T_ch[ch_]] + tl_)

        def layer1_mixer(hf):
            wkv, t_wkv = wload(w_in_block(swwi_d, 2048), "in")
            for half in range(2):
                S.op("dve", lambda half=half: nc.vector.tensor_copy(
                    wk_rep[:, :, :, half * 64:(half + 1) * 64],
                    wkv[:, :, 0:256].rearrange("p c (g d) -> p c g d", d=64)),
                    reads=[t_wkv], writes=t_wkrep)
            if hf == 1:
                S.op("dve", lambda: nc.vector.tensor_copy(kT[:, :, 0:128], kT[:, :, HALF:HALF + 128]),
                     reads=[t_kT], writes=[t_kT])
                S.op("dve", lambda: nc.vector.tensor_copy(vz[:, 0, :, :], vz[:, 8, :, :]), reads=[t_vz], writes=[t_vz])
            for g in range(4):
                for tb in range(2):
                    bk = tb
                    for dc in range(8):
                        S.op("pe", lambda dc=dc, g=g, tb=tb, bk=bk: nc.tensor.matmul(
                            banks[bk][:], wk_rep[:, dc, g, :], hT[:, dc, tb * 512:(tb + 1) * 512],
                            start=(dc == 0), stop=(dc == 7)), reads=t_wkrep + [t_hT[tb]], writes=[bt[bk]])
                    S.op("act", lambda g=g, tb=tb, bk=bk: nc.scalar.copy(
                        kT[:, g, 128 + tb * 512:128 + (tb + 1) * 512], banks[bk][:]), reads=[bt[bk]], writes=[t_kT])
            for tt in range(8):
                bk = 2 + tt % 2
                for dc in range(8):
                    S.op("pe", lambda dc=dc, tt=tt, bk=bk: nc.tensor.matmul(
                        banks[bk][:, 0:256], hT[:, dc, tt * 128:(tt + 1) * 128], wkv[:, dc, 256:512],
                        start=(dc == 0), stop=(dc == 7)), reads=[t_hT[tt // 4], t_wkv], writes=[bt[bk]])
                S.op("act", lambda tt=tt, bk=bk: nc.scalar.copy(
                    vz[:, 1 + tt, :, 64:128], banks[bk][:, 0:256].rearrange("p (g d) -> p g d", d=64)),
                    reads=[bt[bk]], writes=[t_vz])
            kb_lo = -1 if hf == 1 else 0
            scbank = {"n": 0}
            szp2 = [szp, f32v(8, 0, 1024)]
            t_szp2 = [[Tile(f"szp{i_}{k_}") for k_ in range(2)] for i_ in range(2)]
            szp_ch = [T_ch[10], T_ch[8]]
            qTp2 = [qTp, bf16v(9, 0, 1024)]; t_qTp2 = [T_ch[11], T_ch[9]]
            WQ = {}

            ft = [[Tile(f"pTp{a_}_{k_}") for k_ in range(5)] for a_ in range(2)]
            t_rec = [Tile("rec0"), Tile("rec1")]
            recb = [wdiv, d2]
            for a_ in range(2):
                S.op("dve", lambda a_=a_: nc.vector.memset(pTp[a_][:, 0, 0:2], 0.0), writes=[t_pTp[a_], T_ch[11]] + ft[a_])
            S.op("dve", lambda: nc.vector.memset(d2[:, 0:1], 0.0), writes=[T_ch[14]] + t_rec)
            for i_ in range(2):
                S.op("dve", lambda i_=i_: nc.vector.memset(szp2[i_][:, 0:1], 0.0), writes=[szp_ch[i_]] + t_szp2[i_])
            sgq = d2

            def proj_ops(j):
                ops = []
                if j % 4 == 0:
                    WQ["q"] = wload(w_in_block(swwi_d, (j // 4) * 512), "in")
                    WQ["z"] = wload(w_in_block(swwi_d, 2560 + (j // 4) * 512), "in")
                wq, t_wq = WQ["q"]
                wz, t_wz = WQ["z"]
                cs0 = (j % 4) * 128
                pj = j % 2
                for tb in range(2):
                    bk = tb
                    dst = szp2[pj][:, tb * 512:(tb + 1) * 512]
                    for dc in range(8):
                        def z_(dc=dc, tb=tb, bk=bk, dst=dst):
                            S.op("pe", lambda: nc.tensor.matmul(
                                banks[bk][:], wz[:, dc, cs0:cs0 + 128], hT[:, dc, tb * 512:(tb + 1) * 512],
                                start=(dc == 0), stop=(dc == 7)), reads=[t_wz, t_hT[tb]], writes=[bt[bk]])
                            if dc == 7:
                                S.op("act", lambda: nc.scalar.activation(dst, banks[bk][:], AF.Exp, scale=-1.0),
                                     reads=[bt[bk]], writes=[t_szp2[pj][tb]])
                                S.op("act", lambda: nc.scalar.activation(dst, dst, AF.Ln, bias=1.0, scale=1.0),
                                     reads=[t_szp2[pj][tb]], writes=[t_szp2[pj][tb]])
                                S.op("act", lambda: nc.scalar.activation(dst, dst, AF.Exp, scale=-1.0),
                                     reads=[t_szp2[pj][tb]], writes=[t_szp2[pj][tb]])
                                S.op("dve", lambda: nc.vector.tensor_tensor(dst, banks[bk][:], dst, ALU.mult),
                                     reads=[bt[bk], t_szp2[pj][tb]], writes=[t_szp2[pj][tb]])
                        ops.append(z_)
                for tb in range(2):
                    bk = tb
                    for dc in range(8):
                        def q_(dc=dc, tb=tb, bk=bk):
                            S.op("pe", lambda: nc.tensor.matmul(
                                banks[bk][:], wq[:, dc, cs0:cs0 + 128], hT[:, dc, tb * 512:(tb + 1) * 512],
                                start=(dc == 0), stop=(dc == 7)), reads=[t_wq, t_hT[tb]], writes=[bt[bk]])
                            if dc == 7:
                                S.op("dve", lambda: nc.vector.tensor_copy(qTp2[pj][:, tb * 512:(tb + 1) * 512], banks[bk][:]),
                                     reads=[bt[bk]], writes=[t_qTp2[pj]])
                        ops.append(q_)
                return ops

            def scores(j, fill):
                g = j // 4
                pj = j % 2
                qT_, t_qT_ = qTp2[pj], t_qTp2[pj]

                def pop(n):
                    for _ in range(n):
                        if fill:
                            fill.pop(0)()
                for a in range(2):
                    h = 2 * j + a
                    S.op("act", lambda a=a, h=h: nc.scalar.activation(Mh[a], distm[:], AF.Exp, scale=-SLOPES[h]),
                         reads=[t_dist], writes=[t_Mh[a]])
                if kb_lo < 0:
                    for a in range(2):
                        pb = a * 64
                        bk = 2 + a
                        S.op("pe", lambda bk=bk, pb=pb: nc.tensor.matmul(
                            banks[bk][:, 0:128], kT[pb:pb + 64, g, 0:128], qT_[pb:pb + 64, 0:128], start=True, stop=True),
                            reads=[t_kT, t_qT_], writes=[bt[bk]])
                    for a in range(2):
                        bk = 2 + a
                        S.op("act", lambda bk=bk, a=a: nc.scalar.activation(pTh[a], banks[bk][:, 0:128], AF.Exp, scale=0.125),
                             reads=[bt[bk]], writes=[ft[a][4]])
                        S.op("dve", lambda a=a: nc.vector.tensor_tensor(pTh[a], pTh[a], Mh[a][:, 128:256], ALU.mult),
                             reads=[ft[a][4], t_Mh[a]], writes=[ft[a][4]])
                for k0 in range(0, 8, 2):
                    for ii in range(2):
                        for a in range(2):
                            pb = a * 64
                            bk = 2 + a
                            kb = k0 + ii
                            q1 = min(kb + 2, 8)
                            ncol = (q1 - kb) * 128
                            S.op("pe", lambda ii=ii, kb=kb, q1=q1, ncol=ncol, bk=bk, pb=pb: nc.tensor.matmul(
                                banks[bk][:, ii * 256: ii * 256 + ncol],
                                kT[pb:pb + 64, g, (kb + 1) * 128:(kb + 2) * 128],
                                qT_[pb:pb + 64, kb * 128:q1 * 128], start=True, stop=True),
                                reads=[t_kT, t_qT_], writes=[bt[bk]])
                    for a in range(2):
                        bk = 2 + a
                        tk = ft[a][k0 // 2]
                        if k0 < 6:
                            S.op("act", lambda bk=bk, a=a, k0=k0, tk=tk: nc.scalar.activation(
                                pTp[a][:, k0:k0 + 2, :], banks[bk][:, :].rearrange("p (k t) -> p k t", t=256),
                                AF.Exp, scale=0.125), reads=[bt[bk]], writes=[tk])
                            S.op("dve", lambda a=a, k0=k0, tk=tk: nc.vector.tensor_tensor(
                                pTp[a][:, k0:k0 + 2, :], pTp[a][:, k0:k0 + 2, :],
                                Mh[a].unsqueeze(1).to_broadcast([128, 2, 256]), ALU.mult),
                                reads=[tk, t_Mh[a]], writes=[tk])
                        else:
                            for ii, wd in ((0, 256), (1, 128)):
                                S.op("act", lambda bk=bk, a=a, ii=ii, wd=wd, tk=tk: nc.scalar.activation(
                                    pTp[a][:, 6 + ii, 0:wd], banks[bk][:, ii * 256:ii * 256 + wd],
                                    AF.Exp, scale=0.125), reads=[bt[bk]], writes=[tk])
                                S.op("dve", lambda a=a, ii=ii, wd=wd, tk=tk: nc.vector.tensor_tensor(
                                    pTp[a][:, 6 + ii, 0:wd], pTp[a][:, 6 + ii, 0:wd], Mh[a][:, 0:wd], ALU.mult),
                                    reads=[tk, t_Mh[a]], writes=[tk])
                    pop(8)

            def pv(j):
                g = j // 4
                pj = j % 2
                for tb in range(2):
                    for which in range(2):
                        bk = 4 + 2 * tb + which
                        for qi in range(4):
                            n = tb * 4 + qi
                            csl = slice(qi * 128, (qi + 1) * 128)
                            contrib = [(kb, a) for kb in (n - 1, n) if kb >= kb_lo for a in range(2)]
                            for ci, (kb, a) in enumerate(contrib):
                                if which == 0:
                                    lhsT = vz[:, kb + 1, g, 64:192] if a == 0 else vz[:, kb + 1, g, 0:128]
                                else:
                                    lhsT = oz[:, 64:192] if a == 0 else oz[:, 0:128]
                                if kb < 0:
                                    rhs = pTh[a]
                                    tk = ft[a][4]
                                elif kb == n:
                                    rhs = pTp[a][:, kb, 0:128]
                                    tk = ft[a][kb // 2]
                                else:
                                    rhs = pTp[a][:, kb, 128:256]
                                    tk = ft[a][kb // 2]
                                S.op("pe", lambda lhsT=lhsT, rhs=rhs, bk=bk, csl=csl, ci=ci, nct=len(contrib): nc.tensor.matmul(
                                    banks[bk][:, csl], lhsT, rhs, start=(ci == 0), stop=(ci == nct - 1)),
                                    reads=[t_vz, t_oz, tk], writes=[bt[bk]])
                for tb in range(2):
                    nb_, db_ = 4 + 2 * tb, 5 + 2 * tb
                    tsl = slice(tb * 512, (tb + 1) * 512)
                    rb, t_rb = recb[tb], t_rec[tb]
                    S.op("act", lambda db_=db_, rb=rb: nc.scalar.activation(rb, banks[db_][:], AF.Ln, bias=esink[:, j:j + 1], scale=1.0),
                         reads=[bt[db_], t_esink], writes=[t_rb])
                    S.op("act", lambda rb=rb: nc.scalar.activation(rb, rb, AF.Exp, scale=-1.0), reads=[t_rb], writes=[t_rb])
                    S.op("dve", lambda tsl=tsl, rb=rb: nc.vector.tensor_tensor(rb, szp2[pj][:, tsl], rb, ALU.mult),
                         reads=[t_szp2[pj][tb], t_rb], writes=[t_rb])
                    S.op("dve", lambda tsl=tsl, nb_=nb_, rb=rb: nc.vector.tensor_tensor(oT[:, j, tsl], banks[nb_][:], rb, ALU.mult),
                         reads=[bt[nb_], t_rb], writes=[t_oT])

            for f_ in proj_ops(0):
                f_()
            for j in range(16):
                fill = proj_ops(j + 1) if j + 1 < 16 else []
                scores(j, fill)
                while fill:
                    fill.pop(0)()
                pv(j)
            for a_ in range(2):
                S.op("dve", lambda a_=a_: nc.vector.memset(pTp[a_][:, 0, 0:2], 0.0), reads=ft[a_], writes=[t_pTp[a_], T_ch[11]] + ft[a_])
            S.op("dve", lambda: nc.vector.memset(d2[:, 0:1], 0.0), reads=t_rec, writes=[T_ch[14]] + t_rec)
            for i_ in range(2):
                S.op("dve", lambda i_=i_: nc.vector.memset(szp2[i_][:, 0:1], 0.0), reads=t_szp2[i_], writes=[szp_ch[i_]] + t_szp2[i_])

        phase_A(0, 0)
        for hf in range(2):
            layer0_mixer(hf)
            if nlayers == 2:
                phase_C(0, hf, hgwo_d, fillA=phase_A_ops(1, hf))
                layer1_mixer(hf)
                phase_C(1, hf, swwo_d, fillA=(phase_A_ops(0, 1) if hf == 0 else None))
            else:
                phase_C(0, hf, hgwo_d)
                if hf == 0:
                    phase_A(0, 1)

        for t in out_tiles:
            S.wait_tile_dma("sp", t)
        if dbg:
            for t in t_X1:
                S.wait_tile_dma("sp", t)
        build_program.stats = dict(cnt=dict(S.cnt), nwaits=S.nwaits, nsem=S.nsem)
    return nc


_CACHE = {}


def _consts():
    ident = np.eye(128, dtype=np.float32)
    s = np.arange(128)[:, None]
    t = np.arange(128)[None, :]
    tri = (s <= t).astype(np.float32)
    smask = np.ones((128, 512), np.float32)
    smask[:, 0::128] = 0.0
    tr = np.arange(256)[None, :]
    d = (tr - s).astype(np.float32)
    dist = np.where((d >= 0) & (d < 128), d, BIG).astype(np.float32)
    return ident, tri, smask, dist


def make_in_maps(x, c, ada_w, ada_b, ln_g, ln_b, hg_w_in, hg_w_out, hg_lb_logits, hg_norm_w,
                 sw_w_in, sw_w_out, sw_sinks):
    f = lambda a: np.ascontiguousarray(np.asarray(a, dtype=np.float32))
    ident, tri, smask, dist = _consts()
    lbl = f(np.asarray(hg_lb_logits).T.reshape(16, 128, 2).transpose(1, 0, 2))
    nw = f(np.asarray(hg_norm_w)[0].reshape(128, 1))
    sinkp = f(np.repeat(np.asarray(sw_sinks)[0].reshape(16, 2), 64, axis=1).T)
    shared = {
        "ada_w": f(ada_w), "ada_b": f(ada_b), "ln_g": f(ln_g), "ln_b": f(ln_b),
        "hg_w_in": f(np.asarray(hg_w_in)[0]), "hg_w_out": f(np.asarray(hg_w_out)[0]), "lbl": lbl, "nw": nw,
        "sw_w_in": f(np.asarray(sw_w_in)[0]), "sw_w_out": f(np.asarray(sw_w_out)[0]), "sinkp": sinkp,
        "ident": ident, "tri": tri, "smask": smask, "dist": dist,
    }
    maps = []
    for b in range(8):
        m = dict(shared)
        m["x"] = f(np.asarray(x)[b])
        m["cT"] = f(np.asarray(c)[b].reshape(8, 128).T)
        maps.append(m)
    return maps


def kernel(x, c, ada_w, ada_b, ln_g, ln_b, hg_w_in, hg_w_out, hg_lb_logits, hg_norm_w,
           sw_w_in, sw_w_out, sw_sinks):
    if "nc" not in _CACHE:
        _CACHE["nc"] = build_program()
    nc = _CACHE["nc"]
    in_maps = make_in_maps(x, c, ada_w, ada_b, ln_g, ln_b, hg_w_in, hg_w_out, hg_lb_logits, hg_norm_w,
                           sw_w_in, sw_w_out, sw_sinks)
    res = run_bass_kernel_spmd(nc, in_maps, core_ids=list(range(8)))
    out = np.stack([np.asarray(r["out"], dtype=np.float32) for r in res.results], axis=0)
    return out
```
